# Optimizing a Trainium2 kernel written in Bass

```python
import math
import jax, jax.numpy as jnp
from jax import lax
import numpy as np

D_MODEL = 1024
BATCH = 2
SEQ = 8192
DEPTH = 1
DEC_BATCH = 8
DEC_SEQ = 64
PAST_LEN = 2048

CHUNK = 64
Q_BLOCK = 128
EPS = 1e-6
DIFF_HEADS = 8
DIFF_HD = 64
DIFF_QK = DIFF_HEADS * 2 * DIFF_HD
DIFF_V = DIFF_HEADS * 2 * DIFF_HD
DIFF_SCALE = DIFF_HD ** -0.5
GLA_HEADS = 4
GLA_DK = 128
GLA_DV = 256
GLA_K = GLA_HEADS * GLA_DK
GLA_V = GLA_HEADS * GLA_DV
GLA_SCALE = GLA_DK ** -0.5
GK_RANK = 16
GK_NORM = 16.0
MEM_TOKENS = 256
MEM_HEADS = 4
MEM_HD = 256
MEM_W = MEM_HEADS * MEM_HD
MEM_SCALE = MEM_HD ** -0.5
D_FF = 2816
CONV_W = 3
IN_SIZES = (DIFF_QK, DIFF_QK, DIFF_V, GLA_K, GLA_K, GLA_V, GLA_V, GK_RANK, MEM_W, D_MODEL, D_MODEL, D_MODEL)
IN_WIDTH = sum(IN_SIZES)

kernel_name = 'hybrid_diffattn_gla_mem_convffn_stream_step'


def _rms(x, g):
    xf = x.astype(jnp.float32)
    y = xf * lax.rsqrt(jnp.mean(xf * xf, axis=-1, keepdims=True) + EPS)
    return (y * g.astype(jnp.float32)).astype(x.dtype)


def _split_cols(p):
    idx = [int(i) for i in np.cumsum(IN_SIZES)[:-1]]
    return jnp.split(p, idx, axis=-1)


def _heads(t, n, d):
    b, l = t.shape[:2]
    return t.reshape(b, l, n, d).transpose(0, 2, 1, 3).astype(jnp.float32)


def _diff_core(q, k, v, qpos, kpos, lam):
    s = jnp.einsum('bqhmd,bkhmd->bhmqk', q, k).astype(jnp.float32) * DIFF_SCALE
    mask = (kpos[None, :] // CHUNK) <= (qpos[:, None] // CHUNK)
    p = jax.nn.softmax(jnp.where(mask, s, -jnp.inf), axis=-1)
    a = p[:, :, 0] - lam * p[:, :, 1]
    return jnp.einsum('bhqk,bkhe->bqhe', a.astype(v.dtype), v)


def _diff_prompt(q, k, v, lam):
    b, s = q.shape[:2]
    nb = s // Q_BLOCK
    qb = q.reshape(b, nb, Q_BLOCK, DIFF_HEADS, 2, DIFF_HD).transpose(1, 0, 2, 3, 4, 5)
    kpos = jnp.arange(s)

    def blk(args):
        i, qi = args
        return _diff_core(qi, k, v, i * Q_BLOCK + jnp.arange(Q_BLOCK), kpos, lam)

    o = lax.map(blk, (jnp.arange(nb), qb))
    return o.transpose(1, 0, 2, 3, 4).reshape(b, s, DIFF_HEADS, 2 * DIFF_HD)


def _gla_chunk(state, q, k, v, g):
    l = q.shape[2]
    b = jnp.cumsum(g, axis=2)
    causal = jnp.tril(jnp.ones((l, l), dtype=bool))
    dec = jnp.exp(jnp.where(causal[None, None, :, :, None], b[:, :, :, None, :] - b[:, :, None, :, :], -jnp.inf))
    att = jnp.einsum('bhid,bhjd,bhijd->bhij', q, k, dec)
    o = jnp.einsum('bhid,bhdv->bhiv', q * jnp.exp(b), state) + jnp.einsum('bhij,bhjv->bhiv', att, v)
    b_last = b[:, :, -1:, :]
    new_state = jnp.exp(b_last[:, :, 0, :])[..., None] * state + jnp.einsum('bhjd,bhjv->bhdv', k * jnp.exp(b_last - b), v)
    return new_state, o


def _gla_prompt(q, k, v, g):
    b, h, s, _ = q.shape
    nc = s // CHUNK

    def to_chunks(t):
        return t.reshape(b, h, nc, CHUNK, t.shape[-1]).transpose(2, 0, 1, 3, 4)

    s0 = jnp.zeros((b, h, GLA_DK, GLA_DV), jnp.float32)
    s_fin, o = lax.scan(lambda st, xs: _gla_chunk(st, *xs), s0, (to_chunks(q), to_chunks(k), to_chunks(v), to_chunks(g)))
    return o.transpose(1, 2, 0, 3, 4).reshape(b, h, s, GLA_DV), s_fin


def _mem_kv(mem, g_mem, w_mem_kv, kn_mem):
    b, m, _ = mem.shape
    k, v = jnp.split(_rms(mem, g_mem) @ w_mem_kv, 2, axis=-1)
    k = _rms(k.reshape(b, m, MEM_HEADS, MEM_HD), kn_mem)
    return k, v.reshape(b, m, MEM_HEADS, MEM_HD)


def _mem_attn(q, k, v):
    b, l = q.shape[:2]
    s = jnp.einsum('bqhd,bkhd->bhqk', q, k).astype(jnp.float32) * MEM_SCALE
    p = jax.nn.softmax(s, axis=-1)
    return jnp.einsum('bhqk,bkhd->bqhd', p.astype(v.dtype), v).reshape(b, l, MEM_W)


def _mixer_block(x, k_past, v_past, mem_k, mem_v, gla_state, lam, lam_init, p):
    (g_attn, w_in, w_gk2, b_gk, qn_diff, kn_diff, subln_diff, subln_gla, qn_mem,
     w_proj_diff, w_proj_gla, w_proj_mem, w_out) = p
    b, l, _ = x.shape
    h = _rms(x, g_attn)
    qa, ka, va, qb, kb, vb, rb, gk_lr, qm, ga, gb, gm = _split_cols(h @ w_in)
    qa = _rms(qa.reshape(b, l, DIFF_HEADS, 2, DIFF_HD), qn_diff)
    ka = _rms(ka.reshape(b, l, DIFF_HEADS, 2, DIFF_HD), kn_diff)
    va = va.reshape(b, l, DIFF_HEADS, 2 * DIFF_HD)
    qm = _rms(qm.reshape(b, l, MEM_HEADS, MEM_HD), qn_mem)
    gk = jax.nn.log_sigmoid((gk_lr @ w_gk2 + b_gk).astype(jnp.float32)) / GK_NORM
    qh = _heads(qb, GLA_HEADS, GLA_DK) * GLA_SCALE
    kh = _heads(kb, GLA_HEADS, GLA_DK)
    vh = _heads(vb, GLA_HEADS, GLA_DV)
    gh = _heads(gk, GLA_HEADS, GLA_DK)
    if k_past is None:
        oa = _diff_prompt(qa, ka, va, lam)
        ob, gla_new = _gla_prompt(qh, kh, vh, gh)
    else:
        n_past = k_past.shape[1]
        k_all = jnp.concatenate([k_past.astype(ka.dtype), ka], axis=1)
        v_all = jnp.concatenate([v_past.astype(va.dtype), va], axis=1)
        oa = _diff_core(qa, k_all, v_all, n_past + jnp.arange(l), jnp.arange(n_past + l), lam)
        gla_new, ob = _gla_chunk(gla_state.astype(jnp.float32), qh, kh, vh, gh)
    oa = (_rms(oa, subln_diff) * (1.0 - lam_init)).reshape(b, l, DIFF_V)
    ob = _rms(ob.transpose(0, 2, 1, 3), subln_gla).reshape(b, l, GLA_V).astype(x.dtype) * jax.nn.silu(rb)
    om = _mem_attn(qm, mem_k.astype(qm.dtype), mem_v.astype(qm.dtype))
    m = (jax.nn.sigmoid(ga) * (oa @ w_proj_diff)
         + jax.nn.sigmoid(gb) * (ob @ w_proj_gla)
         + jax.nn.sigmoid(gm) * (om @ w_proj_mem))
    return x + m @ w_out, ka, va, gla_new


def _conv_ffn(x, conv_state, g_ffn, w_up, conv_w, conv_b, w_down):
    l = x.shape[1]
    u, v = jnp.split(_rms(x, g_ffn) @ w_up, 2, axis=-1)
    ext = jnp.concatenate([conv_state.astype(u.dtype), u], axis=1)
    uc = conv_b + sum(conv_w[j] * ext[:, j:j + l] for j in range(CONV_W))
    y = (jax.nn.gelu(uc) * v) @ w_down
    return x + y, ext[:, -(CONV_W - 1):]


def setup_inputs(seed: int = 0) -> dict:
    key = jax.random.key(seed)
    ks = iter(jax.random.split(key, 48))

    def nrm(shape, scale):
        return jax.random.normal(next(ks), shape, jnp.float32) * scale

    def gain(shape):
        return 1.0 + nrm(shape, 0.02)

    L = DEPTH
    return {
        'x_prompt': nrm((BATCH, SEQ, D_MODEL), 1.0),
        'x_sample': nrm((DEC_BATCH, DEC_SEQ, D_MODEL), 1.0),
        'mem_prompt': nrm((BATCH, MEM_TOKENS, D_MODEL), 1.0),
        'cache_diff_k': nrm((L, DEC_BATCH, PAST_LEN, DIFF_HEADS, 2, DIFF_HD), 1.0),
        'cache_diff_v': nrm((L, DEC_BATCH, PAST_LEN, DIFF_HEADS, 2 * DIFF_HD), 1.0),
        'cache_mem_k': nrm((L, DEC_BATCH, MEM_TOKENS, MEM_HEADS, MEM_HD), 1.0),
        'cache_mem_v': nrm((L, DEC_BATCH, MEM_TOKENS, MEM_HEADS, MEM_HD), 1.0),
        'state_gla': nrm((L, DEC_BATCH, GLA_HEADS, GLA_DK, GLA_DV), 1.0),
        'state_conv': nrm((L, DEC_BATCH, CONV_W - 1, D_FF), 1.0),
        'g_attn': gain((L, D_MODEL)),
        'w_in': nrm((L, D_MODEL, IN_WIDTH), D_MODEL ** -0.5),
        'w_gk2': nrm((L, GK_RANK, GLA_K), GK_RANK ** -0.5),
        'b_gk': nrm((L, GLA_K), 0.1),
        'qn_diff': gain((L, DIFF_HD)),
        'kn_diff': gain((L, DIFF_HD)),
        'lam_q1': nrm((L, DIFF_HD), 0.1),
        'lam_k1': nrm((L, DIFF_HD), 0.1),
        'lam_q2': nrm((L, DIFF_HD), 0.1),
        'lam_k2': nrm((L, DIFF_HD), 0.1),
        'subln_diff': gain((L, 2 * DIFF_HD)),
        'subln_gla': gain((L, GLA_DV)),
        'g_mem': gain((L, D_MODEL)),
        'w_mem_kv': nrm((L, D_MODEL, 2 * MEM_W), D_MODEL ** -0.5),
        'qn_mem': gain((L, MEM_HD)),
        'kn_mem': gain((L, MEM_HD)),
        'w_proj_diff': nrm((L, DIFF_V, D_MODEL), DIFF_V ** -0.5),
        'w_proj_gla': nrm((L, GLA_V, D_MODEL), GLA_V ** -0.5),
        'w_proj_mem': nrm((L, MEM_W, D_MODEL), MEM_W ** -0.5),
        'w_out': nrm((L, D_MODEL, D_MODEL), D_MODEL ** -0.5),
        'g_ffn': gain((L, D_MODEL)),
        'w_up': nrm((L, D_MODEL, 2 * D_FF), D_MODEL ** -0.5),
        'conv_w': nrm((L, CONV_W, D_FF), CONV_W ** -0.5),
        'conv_b': nrm((L, D_FF), 0.02),
        'w_down': nrm((L, D_FF, D_MODEL), D_FF ** -0.5),
    }


def reference(x_prompt, x_sample, mem_prompt, cache_diff_k, cache_diff_v, cache_mem_k, cache_mem_v,
              state_gla, state_conv, g_attn, w_in, w_gk2, b_gk, qn_diff, kn_diff, lam_q1, lam_k1,
              lam_q2, lam_k2, subln_diff, subln_gla, g_mem, w_mem_kv, qn_mem, kn_mem, w_proj_diff,
              w_proj_gla, w_proj_mem, w_out, g_ffn, w_up, conv_w, conv_b, w_down):
    xp, xs = x_prompt, x_sample
    dkp, dvp, mkp, mvp, gsp, csp = [], [], [], [], [], []
    dks, dvs, gss, css = [], [], [], []
    for l in range(DEPTH):
        lam_init = 0.8 - 0.6 * math.exp(-0.3 * l)
        lam = (jnp.exp(jnp.sum(lam_q1[l].astype(jnp.float32) * lam_k1[l].astype(jnp.float32)))
               - jnp.exp(jnp.sum(lam_q2[l].astype(jnp.float32) * lam_k2[l].astype(jnp.float32)))
               + lam_init)
        p = (g_attn[l], w_in[l], w_gk2[l], b_gk[l], qn_diff[l], kn_diff[l], subln_diff[l], subln_gla[l],
             qn_mem[l], w_proj_diff[l], w_proj_gla[l], w_proj_mem[l], w_out[l])
        fp = (g_ffn[l], w_up[l], conv_w[l], conv_b[l], w_down[l])
        mk, mv = _mem_kv(mem_prompt, g_mem[l], w_mem_kv[l], kn_mem[l])
        xp, kp, vp, sp = _mixer_block(xp, None, None, mk, mv, None, lam, lam_init, p)
        xp, cp = _conv_ffn(xp, jnp.zeros((xp.shape[0], CONV_W - 1, D_FF), xp.dtype), *fp)
        xs, ks_, vs_, ss = _mixer_block(xs, cache_diff_k[l], cache_diff_v[l], cache_mem_k[l], cache_mem_v[l],
                                        state_gla[l], lam, lam_init, p)
        xs, cs = _conv_ffn(xs, state_conv[l], *fp)
        dkp.append(kp); dvp.append(vp); mkp.append(mk); mvp.append(mv); gsp.append(sp); csp.append(cp)
        dks.append(ks_); dvs.append(vs_); gss.append(ss); css.append(cs)
    return (xp, xs, jnp.stack(dkp), jnp.stack(dvp), jnp.stack(mkp), jnp.stack(mvp), jnp.stack(gsp),
            jnp.stack(csp), jnp.stack(dks), jnp.stack(dvs), jnp.stack(gss), jnp.stack(css))
```

```python
import numpy as np
import concourse.bass as bass
import concourse.mybir as mybir
from concourse.bass_utils import run_bass_kernel_spmd

F32 = mybir.dt.float32
BF16 = mybir.dt.bfloat16
AF = mybir.ActivationFunctionType
ALU = mybir.AluOpType
AX = mybir.AxisListType

D = 1024
KC = 8
DFF = 2816
NFC = 22
EPS = 1e-6
COL = dict(qa=0, ka=1024, va=2048, qb=3072, kb=3584, vb=4096, rb=5120, gk=6144, qm=6160, ga=7184, gb=8208, gm=9232)
DIFF_SCALE = 64 ** -0.5
GLA_SCALE = 128 ** -0.5
MEM_SCALE = 256 ** -0.5
LAM_INIT = 0.2
NEG = -30000.0
PIECE = 16
FOLD_WAIT = True
FOLD_DMA = True
CONV_PER_GROUP = 4
STOP_AFTER = None
DBG = 9


class _Stop(Exception):
    pass


class Res:
    __slots__ = ("name", "w", "r", "t", "off", "size", "al", "excl")

    def __init__(self, name, t=None, off=None, size=None):
        self.name = name
        self.w = None
        self.r = []
        self.t = t
        self.off = off
        self.size = size
        self.al = []
        self.excl = False

    def __getitem__(self, k):
        return self.t[k]


class Op:
    __slots__ = ("eng", "fn", "deps", "signal", "event", "prewait", "dma", "single")

    def __init__(self, eng, fn, dma):
        self.eng = eng
        self.fn = fn
        self.deps = []
        self.signal = dma
        self.event = None
        self.prewait = None
        self.dma = dma
        self.single = True


class Sched:
    ENGS = ("pe", "act", "dve", "pool", "sp")

    def __init__(self, nc, n_dma_sems=32):
        self.nc = nc
        self.ops = []
        self.stopped = False
        self.n_dma_sems = n_dma_sems

    def op(self, eng, fn, reads=(), writes=(), dma=False):
        if self.stopped:
            return None
        o = Op(eng, fn, dma)
        deps = []
        for t in reads:
            if t.w is not None:
                deps.append(t.w)
            if t.excl:
                deps.extend(r_ for r_ in t.r if r_.eng != eng)
        for t in writes:
            if t.w is not None:
                deps.append(t.w)
            deps.extend(t.r)
            for a in t.al:
                if a.w is not None:
                    deps.append(a.w)
                deps.extend(a.r)
        seen = set()
        for d in deps:
            if id(d) in seen:
                continue
            seen.add(id(d))
            if d.eng == "pe" and eng == "pe" and not dma and not d.dma:
                continue
            o.deps.append(d)
            d.signal = True
        for t in reads:
            t.r.append(o)
        for t in writes:
            t.w = o
            t.r = []
        self.ops.append(o)
        return o

    def emit(self):
        nc = self.nc
        esem = {e: nc.alloc_semaphore("s_" + e) for e in self.ENGS}
        dsem = [nc.alloc_semaphore("d%d" % i) for i in range(self.n_dma_sems)]
        dtot = [0] * self.n_dma_sems
        ecount = {e: 0 for e in self.ENGS}
        n_sw = 8
        nd = {"pool": 0, "sp": 0}
        for o in self.ops:
            if o.dma:
                if o.eng == "pool":
                    k = nd["pool"] % n_sw
                else:
                    k = n_sw + nd[o.eng] % (self.n_dma_sems - n_sw)
                nd[o.eng] += 1
                o.prewait = (dsem[k], dtot[k], ("d", k))
                dtot[k] += 16
                o.event = (dsem[k], dtot[k], ("d", k))
            elif o.signal:
                ecount[o.eng] += 1
                o.event = (esem[o.eng], ecount[o.eng], ("e", o.eng))
        self.stats = {e: 0 for e in self.ENGS}
        self.nwaits = {e: 0 for e in self.ENGS}
        per = {e: [o for o in self.ops if o.eng == e] for e in self.ENGS}
        with nc.Block() as block:
            def run(ename):
                def body(eng):
                    waited = {}
                    for o in per[ename]:
                        ws = [d.event for d in o.deps]
                        if o.prewait is not None and o.prewait[1] > 0:
                            ws.append(o.prewait)
                        need = {}
                        for (sem, val, key) in ws:
                            if waited.get(key, 0) >= val:
                                continue
                            if need.get(key, (None, 0))[1] < val:
                                need[key] = (sem, val)
                        items = list(need.items())
                        fold = None
                        if FOLD_WAIT and items and o.single and (FOLD_DMA or (not o.dma and ename in ("pe", "act", "dve"))):
                            pick = len(items) - 1
                            for ii_, (k_, _) in enumerate(items):
                                if k_ != ("e", ename):
                                    pick = ii_
                            fold = items.pop(pick)
                        for key, (sem, val) in items:
                            eng.wait_ge(sem, val)
                            waited[key] = val
                            self.nwaits[ename] += 1
                        ins = o.fn(eng)
                        if fold is not None:
                            ins._wait_ge(fold[1][0], fold[1][1])
                            waited[fold[0]] = fold[1][1]
                        self.stats[ename] += 1
                        if o.dma:
                            ins.then_inc(o.event[0], 16)
                        elif o.signal:
                            ins.then_inc(o.event[0], 1)
                    last = {}
                    for o in per[ename]:
                        if o.dma:
                            last[o.event[2]] = o.event
                    for key, (sem, val, _) in last.items():
                        if waited.get(key, 0) < val:
                            eng.wait_ge(sem, val)
                return body
            block.tensor(run("pe"))
            block.scalar(run("act"))
            block.vector(run("dve"))
            block.gpsimd(run("pool"))
            block.sync(run("sp"))


class Rot:
    def __init__(self, items):
        self.items = items
        self.i = 0

    def next(self):
        r = self.items[self.i % len(self.items)]
        self.i += 1
        return r


class Grp:
    def __init__(self, kind, blocks, T):
        self.kind = kind
        self.blocks = blocks
        self.T = T
        self.NB = len(blocks)
        self.NT = self.NB * T


def build(NWIN, PAST, SD=64):
    NOWN = NWIN // 4
    NPRE = NWIN - NOWN - 1
    NTOKW = NWIN * 128
    NPB = PAST // 128
    nc = bass.Bass("TRN2", target_bir_lowering=False)
    S = Sched(nc)

    def din(name, shape):
        return nc.dram_tensor(name, shape, F32, kind="ExternalInput").ap()

    def dout(name, shape):
        return nc.dram_tensor(name, shape, F32, kind="ExternalOutput").ap()

    xw = din("xw", [NTOKW, D])
    kbias_d = din("kbias", [128, NWIN])
    flag_d = din("flag", [128, 1])
    xs = din("xs", [SD, D])
    ck = din("ck", [PAST, D])
    cv = din("cv", [PAST, D])
    cmk = din("cmk", [256, D])
    cmv = din("cmv", [256, D])
    sgla = din("sgla", [512, 256])
    sconv = din("sconv", [2, DFF])
    memx = din("mem", [256, D])
    ident_d = din("ident", [128, 128])
    tri_d = din("tri", [128, 128])
    g_attn = din("g_attn", [1, D])
    w_in = din("w_in", [D, 10256])
    w_gk2 = din("w_gk2", [16, 512])
    b_gk = din("b_gk", [1, 512])
    qn_diff = din("qn_diff", [1, 64])
    kn_diff = din("kn_diff", [1, 64])
    lam_d = [din(n, [1, 64]) for n in ("lam_q1", "lam_k1", "lam_q2", "lam_k2")]
    subln_diff = din("subln_diff", [1, 128])
    subln_gla = din("subln_gla", [1, 256])
    g_mem = din("g_mem", [1, D])
    w_mem_kv = din("w_mem_kv", [D, 2048])
    qn_mem = din("qn_mem", [1, 256])
    kn_mem = din("kn_mem", [1, 256])
    w_pd = din("w_proj_diff", [D, D])
    w_pg = din("w_proj_gla", [D, D])
    w_pm = din("w_proj_mem", [D, D])
    w_out = din("w_out", [D, D])
    g_ffn = din("g_ffn", [1, D])
    w_up = din("w_up", [D, 2 * DFF])
    conv_w = din("conv_w", [3, DFF])
    conv_b = din("conv_b", [1, DFF])
    w_down = din("w_down", [DFF, D])

    y_o = dout("y", [NOWN * 128, D])
    ys_o = dout("ys", [SD, D])
    dk_o = dout("dk", [NOWN * 128, D])
    dv_o = dout("dv", [NOWN * 128, D])
    mk_o = dout("mk", [256, D])
    mv_o = dout("mv", [256, D])
    gs_o = dout("gs", [512, 256])
    cs_o = dout("cs", [2, DFF])
    dks_o = dout("dks", [SD, D])
    dvs_o = dout("dvs", [SD, D])
    gss_o = dout("gss", [512, 256])
    css_o = dout("css", [2, DFF])

    NTOKS = PAST + 128
    kT_p = nc.dram_tensor("kT_p", [8, 128, NTOKW], BF16).ap()
    v_p = nc.dram_tensor("v_p", [NTOKW, 1024], BF16).ap()
    kT_s = nc.dram_tensor("kT_s", [8, 128, NTOKS], BF16).ap()
    v_s = nc.dram_tensor("v_s", [NTOKS, 1024], BF16).ap()

    NWT = 48
    wscr = nc.dram_tensor("wscr", [NWT, 128, KC * 512], BF16).ap()
    wres = [Res("wres%d" % i) for i in range(NWT)]
    wslot = {}

    sb_lo, sb_hi = nc.bump_sbuf(207 * 1024)
    cur = [sb_lo]
    allres = []

    def esz(dt):
        return 4 if dt == F32 else 2

    def tile(name, shape, dt=F32, at=None):
        n = 1
        for s in shape[1:]:
            n *= s
        size = (n * esz(dt) + 31) // 32 * 32
        if at is None:
            off = cur[0]
            cur[0] += size
        else:
            off = at[0]
            at[0] += size
        assert off + size <= sb_hi, ("SBUF overflow", name, off + size - sb_hi)
        t = nc.alloc_sbuf_tensor_at(name, shape, dt, offset=off)
        r = Res(name, t, off, size)
        allres.append(r)
        return r

    ident_f = tile("ident_f", [128, 128])
    identb = tile("identb", [128, 128], BF16)
    tri_f = tile("tri_f", [128, 128])
    ones_f = tile("ones_f", [128, 128])
    trim_f = tile("trim_f", [128, 128])
    ones_b = tile("ones_b", [128, 128], BF16)
    eps_t = tile("eps_t", [128, 1])
    one_t = tile("one_t", [128, 1])
    zero_t = tile("zero_t", [128, 1])
    mhalf_t = tile("mhalf_t", [128, 1])
    gaT = tile("gaT", [128, 8])
    gfT = tile("gfT", [128, 8])
    gmT = tile("gmT", [128, 8])
    qn_t = tile("qn_t", [128, 64])
    kn_t = tile("kn_t", [128, 64])
    sld_t = tile("sld_t", [128, 128])
    slg_t = tile("slg_t", [128, 256])
    qnm_t = tile("qnm_t", [128, 256])
    knm_t = tile("knm_t", [128, 256])
    bgk_t = tile("bgk_t", [128, 512])
    wgk2_t = tile("wgk2_t", [16, 512])
    cwT = tile("cwT", [128, 3, NFC])
    cbT = tile("cbT", [128, NFC])
    lam_t = [tile("lam%d" % i, [128, 64]) for i in range(4)]
    lamtmp = tile("lamtmp", [128, 64])
    lame = tile("lame", [128, 2])
    nlam = tile("nlam", [128, 1])
    kbias = tile("kbias", [128, NWIN])
    flag = tile("flag", [128, 1])
    carry_p = tile("carry", [128, NFC, 2])
    carry_s = tile("carry_s", [128, NFC, 2])
    cur_carry = [carry_p]
    Sst = tile("Sst", [128, 4, 256])
    Sbf = tile("Sbf", [128, 4, 256], BF16)
    memKT = tile("memKT", [128, 4, 2, 256], BF16)
    memV = tile("memV", [128, 2, 4, 256], BF16)
    x_grp = tile("x_grp", [128, 4, D])
    hT = tile("hT", [128, KC, 512], BF16)
    m_acc = tile("m_acc", [128, 4, D])
    oT = [tile("oT%d" % i, [128, KC, 512], BF16) for i in range(2)]
    wbufs = Rot([tile("wb%d" % i, [128, KC, 512], BF16) for i in range(3)])
    sqb = Rot([tile("sq%d" % i, [128, 1024], BF16) for i in range(1)])
    sq5 = Rot([tile("sq5_%d" % i, [128, 512]) for i in range(2)])
    nrm5 = Rot([tile("nrm5_%d" % i, [128, 512]) for i in range(2)])
    f5 = Rot([tile("f5_%d" % i, [128, 512]) for i in range(3)])
    tokbf = Rot([tile("tokbf%d" % i, [128, D], BF16) for i in range(4)])
    tb5 = Rot([tile("tb5_%d" % i, [128, 512], BF16) for i in range(3)])
    kTblk = Rot([tile("kTblk%d" % i, [128, 4, 128], BF16) for i in range(2)])
    small = Rot([tile("small%d" % i, [128, 16]) for i in range(8)])
    gkT = tile("gkT", [16, 512])
    cvt_t = tile("cvt_t", [128, KC, 256], BF16)
    arena0 = cur[0]

    at = [arena0]
    Lst = tile("Lst", [128, 4, 512], at=at)
    Kst = tile("Kst", [128, 4, 512], at=at)
    Qst = tile("Qst", [128, 4, 512], at=at)
    Vst = tile("Vst", [128, 4, 1024], BF16, at=at)
    Rst = tile("Rst", [128, 4, 1024], at=at)
    gexpR = Rot([tile("gexp%d" % i, [128, 512], at=at) for i in range(4)])
    gdecR = Rot([tile("gdec%d" % i, [128, 4], at=at) for i in range(2)])
    gq = tile("gq", [128, 512], BF16, at=at)
    gkt = tile("gkt", [128, 512], BF16, at=at)
    gkhR = Rot([tile("gkh%d" % i, [128, 512], BF16, at=at) for i in range(2)])
    gqT = tile("gqT", [128, 4, 128], BF16, at=at)
    gkT2 = tile("gkT2", [128, 4, 128], BF16, at=at)
    gatt = tile("gatt", [128, 4, 128], BF16, at=at)
    end_gla = at[0]
    at = [arena0]
    QT = tile("QT", [128, 8, 512], BF16, at=at)
    KTp = Rot([tile("KTp%d" % i, [128, 2048], BF16, at=at) for i in range(3)])
    Vp = Rot([tile("Vp%d" % i, [128, 16, 129], BF16, at=at) for i in range(3)])
    Pt = Rot([tile("Pt%d" % i, [128, 2, 512], BF16, at=at) for i in range(3)])
    oa0 = Rot([tile("oa0_%d" % i, [128, 128], at=at) for i in range(2)])
    oa1 = Rot([tile("oa1_%d" % i, [128, 128], at=at) for i in range(2)])
    oab = Rot([tile("oab_%d" % i, [128, 128], BF16, at=at) for i in range(8)])
    osbR = Rot([tile("osb_%d" % i, [128, 258], at=at) for i in range(8)])
    end_att = at[0]
    at = [arena0]
    QMT = tile("QMT", [128, 8, 512], BF16, at=at)
    pm = Rot([tile("pm%d" % i, [128, 2, 512], BF16, at=at) for i in range(2)])
    rl = Rot([tile("rl%d" % i, [128, 512], at=at) for i in range(2)])
    end_mem = at[0]
    at = [arena0]
    actT = tile("actT", [128, NFC, 512], BF16, at=at)
    ub = Rot([tile("ub%d" % i, [128, 516], at=at) for i in range(2)])
    t1b = Rot([tile("t1b%d" % i, [128, 512], at=at) for i in range(2)])
    end_ffn = at[0]
    print("SBUF: persistent %d, arena gla %d att %d mem %d ffn %d, limit %d" % (
        arena0 - sb_lo, end_gla - arena0, end_att - arena0, end_mem - arena0, end_ffn - arena0, sb_hi - arena0))

    for i, a in enumerate(allres):
        for b in allres[i + 1:]:
            if a.off < b.off + b.size and b.off < a.off + a.size:
                a.al.append(b)
                b.al.append(a)

    dbl = [nc.alloc_psum_tensor("dbank%d" % i, [128, 1024], F32) for i in range(4)]
    banks = [Res("bank%d" % i, dbl[i // 2][:, (i % 2) * 512:(i % 2 + 1) * 512]) for i in range(8)]
    for b_ in banks:
        b_.excl = True
    psg = Rot(banks[0:4])
    psS = Rot([(banks[0], banks[1], dbl[0]), (banks[2], banks[3], dbl[1])])
    psO = banks[4:8]

    def act(out, in_, func, r, w, bias=None, scale=1.0, accum=None):
        kw = {}
        if bias is not None:
            kw["bias"] = bias
        if accum is not None:
            kw["accum_out"] = accum
        S.op("act", lambda e: e.activation(out=out, in_=in_, func=func, scale=scale, **kw), r, w)

    def tt(out, in0, in1, op, r, w, eng="dve"):
        S.op(eng, lambda e: e.tensor_tensor(out=out, in0=in0, in1=in1, op=op), r, w)

    def ts(out, in0, s1, s2, op0, op1, r, w):
        if s2 is None:
            S.op("dve", lambda e: e.tensor_scalar(out=out, in0=in0, scalar1=s1, scalar2=None, op0=op0), r, w)
        else:
            S.op("dve", lambda e: e.tensor_scalar(out=out, in0=in0, scalar1=s1, scalar2=s2, op0=op0, op1=op1), r, w)

    def stt(out, in0, scalar, in1, op0, op1, r, w):
        S.op("dve", lambda e: e.scalar_tensor_tensor(out=out, in0=in0, scalar=scalar, in1=in1, op0=op0, op1=op1), r, w)

    def vcopy(out, in_, r, w):
        S.op("dve", lambda e: e.tensor_copy(out=out, in_=in_), r, w)

    def acopy(out, in_, r, w):
        S.op("act", lambda e: e.copy(out=out, in_=in_), r, w)

    def vreduce(out, in_, r, w):
        S.op("dve", lambda e: e.tensor_reduce(out=out, in_=in_, axis=AX.X, op=ALU.add), r, w)

    def vrecip(out, in_, r, w):
        S.op("dve", lambda e: e.reciprocal(out=out, in_=in_), r, w)

    def vmemset(ap, val, w):
        S.op("dve", lambda e: e.memset(ap, val), (), w)

    def mm(out, pairs, r, w, start=True, stop=True, skipgc=False):
        def fn(e):
            n = len(pairs)
            ins = None
            for i, (l, rh) in enumerate(pairs):
                if skipgc:
                    ins = e.matmul(out, lhsT=l, rhs=rh, start=(start and i == 0), stop=(stop and i == n - 1),
                                   skip_group_check=True)
                else:
                    ins = e.matmul(out, lhsT=l, rhs=rh, start=(start and i == 0), stop=(stop and i == n - 1))
            return ins
        o_ = S.op("pe", fn, r, w)
        if o_ is not None:
            o_.single = (len(pairs) == 1)

    def dma(eng, out, in_, r, w, slow=False):
        if slow:
            S.op(eng, lambda e: e.dma_start(out=out, in_=in_, allow_slow_non_contiguous=True), r, w, dma=True)
        else:
            S.op(eng, lambda e: e.dma_start(out=out, in_=in_), r, w, dma=True)

    def g3(ap, g):
        return ap.rearrange("p (g d) -> p g d", g=g)

    conv_on = [False]

    def load_w(w2d, r0, nk, c0, ncols):
        wb = load_w_(w2d, r0, nk, c0, ncols)
        if conv_on[0]:
            convert_some(1)
        return wb

    def load_w_(w2d, r0, nk, c0, ncols):
        wb = wbufs.next()
        key = (id(w2d), r0, nk, c0, ncols)
        if key in wslot:
            sl = wslot[key]
            dma("pool", wb[:, 0:nk, 0:ncols],
                wscr[sl, :, 0:nk * ncols].rearrange("p (k c) -> p k c", k=nk), [wres[sl]], [wb])
        else:
            dma("pool", wb[:, 0:nk, 0:ncols],
                w2d[r0:r0 + nk * 128, c0:c0 + ncols].rearrange("(k p) c -> p k c", p=128), (), [wb])
        return wb

    def weight_keys():
        ks = []

        def k(w, r0, nk, c0, n_):
            ks.append((w, r0, nk, c0, n_))
        for c_ in (COL["gk"],):
            k(w_in, 0, KC, c_, 16)
        for nm in ("kb", "vb", "ka", "va"):
            k(w_in, 0, KC, COL[nm], 512)
            if nm != "kb":
                k(w_in, 0, KC, COL[nm] + 512, 512)
        k(w_in, 0, KC, COL["qb"], 512)
        k(w_in, 0, KC, COL["rb"], 512)
        k(w_in, 0, KC, COL["rb"] + 512, 512)
        for (gname, wp) in (("gb", w_pg), ("ga", w_pd), ("gm", w_pm)):
            if gname == "ga":
                k(w_in, 0, KC, COL["qa"], 512)
                k(w_in, 0, KC, COL["qa"] + 512, 512)
            if gname == "gm":
                k(w_in, 0, KC, COL["qm"], 512)
                k(w_in, 0, KC, COL["qm"] + 512, 512)
            for ct in range(2):
                k(w_in, 0, KC, COL[gname] + ct * 512, 512)
                k(wp, 0, KC, ct * 512, 512)
        for ct in range(2):
            k(w_out, 0, KC, ct * 512, 512)
        for t6 in range(6):
            ncol = 512 if t6 < 5 else 256
            k(w_up, 0, KC, t6 * 512, ncol)
            k(w_up, 0, KC, DFF + t6 * 512, ncol)
        for ct in range(2):
            for (f0, nk) in ((0, 8), (8, 8), (16, 6)):
                k(w_down, f0 * 128, nk, ct * 512, 512)
        return ks

    wk_pending = weight_keys()
    assert len(wk_pending) <= NWT, len(wk_pending)

    cv_state = {"slots": 0, "half": 0}

    def convert_some(n):
        while n > 0 and wk_pending:
            (w2d, r0, nk, c0, ncols) = wk_pending[0]
            sl = cv_state["slots"]
            h0 = cv_state["half"] * 256
            hn = min(256, ncols - h0)
            dma("pool", cvt_t[:, 0:nk, 0:hn],
                w2d[r0:r0 + nk * 128, c0 + h0:c0 + h0 + hn].rearrange("(k p) c -> p k c", p=128), (), [cvt_t])
            dma("sp", wscr[sl, :, 0:nk * ncols].rearrange("p (k c) -> p k c", k=nk)[:, :, h0:h0 + hn],
                cvt_t[:, 0:nk, 0:hn], [cvt_t], [wres[sl]])
            if h0 + hn >= ncols:
                wk_pending.pop(0)
                wslot[(id(w2d), r0, nk, c0, ncols)] = sl
                cv_state["slots"] += 1
                cv_state["half"] = 0
            else:
                cv_state["half"] += 1
            n -= 1

    def transpose_to(ps, col0, src_ap, src_res, Tn, ncol):
        mm(ps[0:ncol, col0:col0 + Tn], [(src_ap, identb[0:Tn, 0:Tn])], [src_res, identb], [ps])

    def headnorm(src, src_res, T, n, gs, gain, dst, dst_res, extra=None):
        ng = n // gs
        sq = sq5.next()
        act(sq[0:T, 0:n], src, AF.Square, [src_res], [sq])
        ss = small.next()
        vreduce(ss[0:T, 0:ng], g3(sq[0:T, 0:n], ng), [sq], [ss])
        act(ss[0:T, 0:ng], ss[0:T, 0:ng], AF.Ln, [ss, eps_t], [ss], bias=eps_t[0:T, :], scale=1.0 / gs)
        act(ss[0:T, 0:ng], ss[0:T, 0:ng], AF.Exp, [ss], [ss], scale=-0.5)
        nr = nrm5.next()
        tt(g3(nr[0:T, 0:n], ng), g3(src, ng), ss[0:T, 0:ng].unsqueeze(2).to_broadcast([T, ng, gs]), ALU.mult,
           [src_res, ss], [nr])
        gb = gain[0:T, 0:gs].unsqueeze(1).to_broadcast([T, ng, gs])
        if extra is None:
            tt(g3(dst, ng), g3(nr[0:T, 0:n], ng), gb, ALU.mult, [nr, gain], [dst_res])
        else:
            ex_ap, ex_res = extra
            tt(g3(nr[0:T, 0:n], ng), g3(nr[0:T, 0:n], ng), gb, ALU.mult, [nr, gain], [nr])
            tt(dst, nr[0:T, 0:n], ex_ap, ALU.mult, [nr, ex_res], [dst_res])

    def norm_transpose(src_ap_fn, src_res, g, gT_t, dst):
        T = g.T
        xbs = []
        for i in range(g.NB):
            src = src_ap_fn(i)
            sq = sqb.next()
            ss = small.next()
            act(sq[0:T, :], src, AF.Square, [src_res], [sq, ss], accum=ss[0:T, 0:1])
            act(ss[0:T, 0:1], ss[0:T, 0:1], AF.Ln, [ss, eps_t], [ss], bias=eps_t[0:T, :], scale=1.0 / D)
            act(ss[0:T, 0:1], ss[0:T, 0:1], AF.Exp, [ss], [ss], scale=-0.5)
            xb = tokbf.next()
            ts(xb[0:T, :], src, ss[0:T, 0:1], None, ALU.mult, None, [src_res, ss], [xb])
            xbs.append(xb)
        for i in range(g.NB):
            xb = xbs[i]
            for half in range(2):
                ps = psg.next()
                for q in range(4):
                    kc = half * 4 + q
                    transpose_to(ps, q * T, xb[0:T, kc * 128:(kc + 1) * 128], xb, T, 128)
                tt(dst[:, half * 4:half * 4 + 4, i * T:(i + 1) * T],
                   ps[:, 0:4 * T].rearrange("p (q t) -> p q t", q=4),
                   gT_t[:, half * 4:half * 4 + 4].unsqueeze(2).to_broadcast([128, 4, T]), ALU.mult,
                   [ps, gT_t], [dst])

    def proj_tok(hsrc, g, w2d, c0, ntot, consumer):
        T = g.T
        ct = 0
        c = 0
        pend = [None]
        while c < ntot:
            ncol = min(512, ntot - c)
            wb = load_w(w2d, 0, KC, c0 + c, ncol)
            for i in range(g.NB):
                ps = psg.next()
                mm(ps[0:T, 0:ncol],
                   [(hsrc[:, kc, i * T:(i + 1) * T], wb[:, kc, 0:ncol]) for kc in range(KC)],
                   [hsrc, wb], [ps])
                tail = consumer(ct, i, ps, ncol)
                if pend[0] is not None:
                    pend[0]()
                pend[0] = tail
            c += ncol
            ct += 1
        if pend[0] is not None:
            pend[0]()

    dma("sp", ident_f[:, :], ident_d[:, :], (), [ident_f])
    dma("sp", tri_f[:, :], tri_d[:, :], (), [tri_f])
    vcopy(identb[:, :], ident_f[:, :], [ident_f], [identb])
    ts(trim_f[:, :], tri_f[:, :], -1.0, None, ALU.add, None, [tri_f], [trim_f])
    vmemset(ones_f[:, :], 1.0, [ones_f])
    vmemset(ones_b[:, :], 1.0, [ones_b])
    vmemset(eps_t[:, :], EPS, [eps_t])
    vmemset(one_t[:, :], 1.0, [one_t])
    vmemset(zero_t[:, :], 0.0, [zero_t])
    vmemset(mhalf_t[:, :], -0.5, [mhalf_t])
    for (t_, d_) in ((gaT, g_attn), (gfT, g_ffn), (gmT, g_mem)):
        dma("sp", t_[:, :], d_[0, :].rearrange("(k p) -> p k", p=128), (), [t_], slow=True)
    for (t_, d_, n_) in ((qn_t, qn_diff, 64), (kn_t, kn_diff, 64), (sld_t, subln_diff, 128), (slg_t, subln_gla, 256),
                         (qnm_t, qn_mem, 256), (knm_t, kn_mem, 256), (bgk_t, b_gk, 512)):
        dma("sp", t_[:, :], d_[0:1, :].partition_broadcast(128), (), [t_])
    for i in range(4):
        dma("sp", lam_t[i][:, :], lam_d[i][0:1, :].partition_broadcast(128), (), [lam_t[i]])
    dma("sp", wgk2_t[:, :], w_gk2[:, :], (), [wgk2_t])
    for j in range(3):
        dma("sp", cwT[:, j, :], conv_w[j, :].rearrange("(c p) -> p c", p=128), (), [cwT], slow=True)
    dma("sp", cbT[:, :], conv_b[0, :].rearrange("(c p) -> p c", p=128), (), [cbT], slow=True)
    dma("sp", kbias[:, :], kbias_d[:, :], (), [kbias])
    dma("sp", flag[:, :], flag_d[:, :], (), [flag])
    for i in range(2):
        tt(lamtmp[:, :], lam_t[2 * i][:, :], lam_t[2 * i + 1][:, :], ALU.mult, [lam_t[2 * i], lam_t[2 * i + 1]], [lamtmp])
        vreduce(lame[:, i:i + 1], lamtmp[:, :], [lamtmp], [lame])
    act(lame[:, :], lame[:, :], AF.Exp, [lame], [lame])
    tt(nlam[:, :], lame[:, 1:2], lame[:, 0:1], ALU.subtract, [lame], [nlam])
    ts(nlam[:, :], nlam[:, :], -LAM_INIT, None, ALU.add, None, [nlam], [nlam])
    ts(sld_t[:, :], sld_t[:, :], 1.0 - LAM_INIT, None, ALU.mult, None, [sld_t], [sld_t])
    class Seq:
        pass

    sp_ = Seq()
    sp_.kT, sp_.v = kT_p, v_p
    sp_.kres = [Res("kres_p%d" % i) for i in range(NWIN)]
    sp_.vres = [Res("vres_p%d" % i) for i in range(NWIN)]
    ss_ = Seq()
    ss_.kT, ss_.v = kT_s, v_s
    ss_.kres = [Res("kres_s%d" % i) for i in range(NPB + 1)]
    ss_.vres = [Res("vres_s%d" % i) for i in range(NPB + 1)]

    def mem_kv_prompt():
        g = Grp("mem", [0, 1], 128)
        for i in range(2):
            dma("sp", x_grp[:, i, :], memx[i * 128:(i + 1) * 128, :], (), [x_grp])
        norm_transpose(lambda i: x_grp[:, i, :], x_grp, g, gmT, hT)

        def cons(ct, i, ps, ncol):
            if ct < 2:
                kf = f5.next()
                headnorm(ps[:, 0:512], ps, 128, 512, 256, knm_t, kf[:, :], kf)
                dma("sp", mk_o[i * 128:(i + 1) * 128, ct * 512:(ct + 1) * 512], kf[:, :], [kf], ())
                kb_ = tb5.next()
                acopy(kb_[:, :], kf[:, :], [kf], [kb_])
                ps2 = psg.next()
                for q in range(4):
                    transpose_to(ps2, q * 128, kb_[:, q * 128:(q + 1) * 128], kb_, 128, 128)
                vcopy(memKT[:, 2 * ct:2 * ct + 2, :, i * 128:(i + 1) * 128],
                      ps2[:, 0:512].rearrange("p (h c t) -> p h c t", h=2, c=2), [ps2], [memKT])
            else:
                vf = f5.next()
                acopy(vf[:, :], ps[:, 0:512], [ps], [vf])
                dma("sp", mv_o[i * 128:(i + 1) * 128, (ct - 2) * 512:(ct - 1) * 512], vf[:, :], [vf], ())
                vcopy(memV[:, i, 2 * (ct - 2):2 * (ct - 2) + 2, :], vf[:, :].rearrange("p (h e) -> p h e", h=2),
                      [vf], [memV])
        proj_tok(hT, g, w_mem_kv, 0, 2048, cons)

    def sample_prep_block(pb):
        if True:
            kb_ = tokbf.next()
            dma("pool", kb_[:, :], ck[pb * 128:(pb + 1) * 128, :], (), [kb_])
            for half in range(2):
                ps = psg.next()
                for q in range(4):
                    h = half * 4 + q
                    transpose_to(ps, q * 128, kb_[:, h * 128:(h + 1) * 128], kb_, 128, 128)
                kt_ = kTblk.next()
                vcopy(kt_[:, :, :], ps[:, 0:512].rearrange("p (q t) -> p q t", q=4), [ps], [kt_])
                dma("sp", kT_s[half * 4:half * 4 + 4, :, pb * 128:(pb + 1) * 128].rearrange("h d t -> d h t"),
                    kt_[:, :, :], [kt_], [ss_.kres[pb]])
            vb_ = tokbf.next()
            dma("pool", vb_[:, :], cv[pb * 128:(pb + 1) * 128, :], (), [vb_])
            dma("sp", v_s[pb * 128:(pb + 1) * 128, :], vb_[:, :], [vb_], [ss_.vres[pb]])

    def sample_prep_mem():
        for mb in range(2):
            kb_ = tokbf.next()
            dma("pool", kb_[:, :], cmk[mb * 128:(mb + 1) * 128, :], (), [kb_])
            for half in range(2):
                ps = psg.next()
                for q in range(4):
                    kc = half * 4 + q
                    transpose_to(ps, q * 128, kb_[:, kc * 128:(kc + 1) * 128], kb_, 128, 128)
                vcopy(memKT[:, 2 * half:2 * half + 2, :, mb * 128:(mb + 1) * 128],
                      ps[:, 0:512].rearrange("p (h c t) -> p h c t", h=2, c=2), [ps], [memKT])
            dma("pool", memV[:, mb, :, :].rearrange("p h e -> p (h e)"), cmv[mb * 128:(mb + 1) * 128, :], (), [memV])

    def gla_stage(g, full):
        T, NB, NT = g.T, g.NB, g.NT
        wb = load_w(w_in, 0, KC, COL["gk"], 16)
        ps = psg.next()
        mm(ps[0:16, 0:NT], [(wb[:, kc, 0:16], hT[:, kc, 0:NT]) for kc in range(KC)], [wb, hT], [ps])
        vcopy(gkT[:, 0:NT], ps[0:16, 0:NT], [ps], [gkT])
        for i in range(NB):
            ps = psg.next()
            mm(ps[0:T, 0:512], [(gkT[0:16, i * T:(i + 1) * T], wgk2_t[0:16, :])], [gkT, wgk2_t], [ps])
            xg = f5.next()
            tt(xg[0:T, :], ps[0:T, 0:512], bgk_t[0:T, :], ALU.add, [ps, bgk_t], [xg])
            act(xg[0:T, :], xg[0:T, :], AF.Exp, [xg], [xg], scale=-1.0)
            act(Lst[0:T, i, :], xg[0:T, :], AF.Ln, [xg, one_t], [Lst], bias=one_t[0:T, :])

        def cons_k(ct, i, ps, ncol):
            acopy(Kst[0:T, i, :], ps[0:T, 0:512], [ps], [Kst])

        def cons_q(ct, i, ps, ncol):
            acopy(Qst[0:T, i, :], ps[0:T, 0:512], [ps], [Qst])

        def cons_v(ct, i, ps, ncol):
            vcopy(Vst[0:T, i, ct * 512:(ct + 1) * 512], ps[0:T, 0:512], [ps], [Vst])

        def cons_r(ct, i, ps, ncol):
            act(Rst[0:T, i, ct * 512:(ct + 1) * 512], ps[0:T, 0:512], AF.Silu, [ps], [Rst])

        proj_tok(hT, g, w_in, COL["kb"], 512, cons_k)
        if full:
            proj_tok(hT, g, w_in, COL["qb"], 512, cons_q)
        proj_tok(hT, g, w_in, COL["vb"], 1024, cons_v)
        if full:
            proj_tok(hT, g, w_in, COL["rb"], 1024, cons_r)

        obT = oT[0]
        if full:
            acopy(Sbf[:, :, :], Sst[:, :, :], [Sst], [Sbf])
        for i in range(NB):
            L_i = Lst[0:T, i, :]
            psE = psg.next()
            mm(psE[0:T, 0:512], [(trim_f[0:T, 0:T], L_i)], [trim_f, Lst], [psE])
            psD = psg.next()
            for h in range(4):
                mm(psD[:, h:h + 1], [(Lst[0:T, i, h * 128:(h + 1) * 128], ones_f[0:T, 0:1])], [Lst, ones_f], [psD])
            explb = gexpR.next()
            gdec = gdecR.next()
            gkh = gkhR.next()
            act(explb[0:T, :], psE[0:T, 0:512], AF.Exp, [psE], [explb], scale=1.0 / 16)
            act(gdec[:, 0:4], psD[:, 0:4], AF.Exp, [psD], [gdec], scale=-1.0 / 16)
            if full:
                psB = psg.next()
                mm(psB[0:T, 0:512], [(tri_f[0:T, 0:T], L_i)], [tri_f, Lst], [psB])
                expb = gexpR.next()
                expnb = gexpR.next()
                act(expb[0:T, :], psB[0:T, 0:512], AF.Exp, [psB], [expb], scale=-1.0 / 16)
                act(expnb[0:T, :], psB[0:T, 0:512], AF.Exp, [psB], [expnb], scale=1.0 / 16)
            tt(gkh[0:T, :], Kst[0:T, i, :], explb[0:T, :], ALU.mult, [Kst, explb], [gkh])
            if full:
                stt(gq[0:T, :], Qst[0:T, i, :], GLA_SCALE, expb[0:T, :], ALU.mult, ALU.mult, [Qst, expb], [gq])
                tt(gkt[0:T, :], Kst[0:T, i, :], expnb[0:T, :], ALU.mult, [Kst, expnb], [gkt])
                psQ = psg.next()
                for h in range(4):
                    transpose_to(psQ, h * T, gq[0:T, h * 128:(h + 1) * 128], gq, T, 128)
                acopy(gqT[:, :, 0:T], psQ[:, 0:4 * T].rearrange("p (h t) -> p h t", h=4), [psQ], [gqT])
                psK = psg.next()
                for h in range(4):
                    transpose_to(psK, h * T, gkt[0:T, h * 128:(h + 1) * 128], gkt, T, 128)
                vcopy(gkT2[:, :, 0:T], psK[:, 0:4 * T].rearrange("p (h t) -> p h t", h=4), [psK], [gkT2])
                psA = psg.next()
                for h in range(4):
                    mm(psA[0:T, h * T:(h + 1) * T], [(gkT2[:, h, 0:T], gqT[:, h, 0:T])], [gkT2, gqT], [psA])
                tt(gatt[0:T, :, 0:T], psA[0:T, 0:4 * T].rearrange("p (h t) -> p h t", h=4),
                   tri_f[0:T, 0:T].unsqueeze(1).to_broadcast([T, 4, T]), ALU.mult, [psA, tri_f], [gatt])
                for hb in range(2):
                    pso = psg.next()
                    for hh in range(2):
                        h = hb * 2 + hh
                        mm(pso[0:T, hh * 256:(hh + 1) * 256],
                           [(gqT[:, h, 0:T], Sbf[:, h, :]), (gatt[0:T, h, 0:T], Vst[0:T, i, h * 256:(h + 1) * 256])],
                           [gqT, Sbf, gatt, Vst], [pso])
                    ob = tb5.next()
                    headnorm(pso[0:T, 0:512], pso, T, 512, 256, slg_t, ob[0:T, :], ob,
                             extra=(Rst[0:T, i, hb * 512:(hb + 1) * 512], Rst))
                    pst = psg.next()
                    for q in range(4):
                        transpose_to(pst, q * T, ob[0:T, q * 128:(q + 1) * 128], ob, T, 128)
                    acopy(obT[:, hb * 4:hb * 4 + 4, i * T:(i + 1) * T],
                          pst[:, 0:4 * T].rearrange("p (q t) -> p q t", q=4), [pst], [obT])
            for hb in range(2):
                pss = psg.next()
                for hh in range(2):
                    h = hb * 2 + hh
                    mm(pss[:, hh * 256:(hh + 1) * 256],
                       [(gkh[0:T, h * 128:(h + 1) * 128], Vst[0:T, i, h * 256:(h + 1) * 256])], [gkh, Vst], [pss])
                for hh in range(2):
                    h = hb * 2 + hh
                    stt(Sst[:, h, :], Sst[:, h, :], gdec[:, h:h + 1], pss[:, hh * 256:(hh + 1) * 256],
                        ALU.mult, ALU.add, [Sst, gdec, pss], [Sst])
            if full:
                acopy(Sbf[:, :, :], Sst[:, :, :], [Sst], [Sbf])
        return obT

    def branch_proj(g, oTt, wp, gcol, first):
        T = g.T
        for ct in range(2):
            wg = load_w(w_in, 0, KC, gcol + ct * 512, 512)
            wpp = load_w(wp, 0, KC, ct * 512, 512)
            for i in range(g.NB):
                psa = psg.next()
                mm(psa[0:T, :], [(hT[:, kc, i * T:(i + 1) * T], wg[:, kc, :]) for kc in range(KC)], [hT, wg], [psa])
                psb = psg.next()
                mm(psb[0:T, :], [(oTt[:, kc, i * T:(i + 1) * T], wpp[:, kc, :]) for kc in range(KC)], [oTt, wpp], [psb])
                sg = f5.next()
                act(sg[0:T, :], psa[0:T, :], AF.Sigmoid, [psa], [sg])
                dst = m_acc[0:T, i, ct * 512:(ct + 1) * 512]
                if first:
                    tt(dst, sg[0:T, :], psb[0:T, :], ALU.mult, [sg, psb], [m_acc])
                else:
                    tt(sg[0:T, :], sg[0:T, :], psb[0:T, :], ALU.mult, [sg, psb], [sg])
                    tt(dst, dst, sg[0:T, :], ALU.add, [m_acc, sg], [m_acc])

    def diff_kv(g, seq, tok0_of, kres_of, outs):
        T = g.T

        def cons_k(ct, i, ps, ncol):
            kb_ = tb5.next()
            if outs is not None:
                kf = f5.next()
                headnorm(ps[0:T, 0:512], ps, T, 512, 64, kn_t, kf[0:T, :], kf)
                dma("sp", outs[0][outs[2](i):outs[2](i) + T, ct * 512:(ct + 1) * 512], kf[0:T, :], [kf], ())
                acopy(kb_[0:T, :], kf[0:T, :], [kf], [kb_])
            else:
                headnorm(ps[0:T, 0:512], ps, T, 512, 64, kn_t, kb_[0:T, :], kb_)

            def tail():
                ps2 = psg.next()
                for q in range(4):
                    transpose_to(ps2, q * T, kb_[0:T, q * 128:(q + 1) * 128], kb_, T, 128)
                kt_ = kTblk.next()
                vcopy(kt_[:, :, 0:T], ps2[:, 0:4 * T].rearrange("p (q t) -> p q t", q=4), [ps2], [kt_])
                t0 = tok0_of(i)
                dma("sp", seq.kT[ct * 4:ct * 4 + 4, :, t0:t0 + T].rearrange("h d t -> d h t"), kt_[:, :, 0:T],
                    [kt_], [seq.kres[kres_of(i)]])
            return tail

        def cons_v(ct, i, ps, ncol):
            t0 = tok0_of(i)
            if outs is not None:
                vf = f5.next()
                acopy(vf[0:T, :], ps[0:T, 0:512], [ps], [vf])
                dma("sp", outs[1][outs[2](i):outs[2](i) + T, ct * 512:(ct + 1) * 512], vf[0:T, :], [vf], ())
            vb_ = tb5.next()
            vcopy(vb_[0:T, :], ps[0:T, 0:512], [ps], [vb_])
            dma("sp", seq.v[t0:t0 + T, ct * 512:(ct + 1) * 512], vb_[0:T, :], [vb_], [seq.vres[kres_of(i)]])

        proj_tok(hT, g, w_in, COL["ka"], 1024, cons_k)
        proj_tok(hT, g, w_in, COL["va"], 1024, cons_v)

    def diff_q(g):
        T = g.T

        def cons_q(ct, i, ps, ncol):
            qb_ = tb5.next()
            headnorm(ps[0:T, 0:512], ps, T, 512, 64, qn_t, qb_[0:T, :], qb_)

            def tail():
                ps2 = psg.next()
                for q in range(4):
                    transpose_to(ps2, q * T, qb_[0:T, q * 128:(q + 1) * 128], qb_, T, 128)
                vcopy(QT[:, ct * 4:ct * 4 + 4, i * T:(i + 1) * T], ps2[:, 0:4 * T].rearrange("p (q t) -> p q t", q=4),
                      [ps2], [QT])
            return tail
        proj_tok(hT, g, w_in, COL["qa"], 1024, cons_q)

    def diff_attention(g, seq, keyblocks_of, oaT):
        T = g.T
        nq = g.NB
        QW = nq * T
        for v_ in Vp.items:
            vmemset(v_[:, :, 128:129], 1.0, [v_])
        kbs = keyblocks_of()

        def vis(qi, r):
            return r is None or qi >= r
        first_for, last_for = {}, {}
        for idx, kb in enumerate(kbs):
            for qi in range(nq):
                if vis(qi, kb[3]):
                    last_for[qi] = idx
                    first_for.setdefault(qi, idx)
        fin_tails = []
        fin_chain = []

        def flush_tails():
            while fin_tails:
                fin_tails.pop(0)()

        def flush_chain():
            while fin_chain:
                fin_chain.pop(0)()

        for h in range(8):
            ktp = vp = None
            pend_pv = [None]

            def do_pv(st, h=h):
                p_, vp_, j16_, r_, nk_, idx_ = st
                for qi in range(nq):
                    if not vis(qi, r_):
                        continue
                    for m in range(2):
                        mm(psO[qi][0:T, m * 129:(m + 1) * 129],
                           [(p_[0:nk_, m, qi * T:(qi + 1) * T], vp_[0:nk_, j16_, 0:129])], [p_, vp_], [psO[qi]],
                           start=(idx_ == first_for[qi] and m == 0), stop=(idx_ == last_for[qi]), skipgc=True)

            for idx, (kbi, tok0, nk, r, bias_ap, bias_res) in enumerate(kbs):
                if idx % PIECE == 0:
                    piece = kbs[idx:idx + PIECE]
                    ktp = KTp.next()
                    vp = Vp.next()
                    pt0 = piece[0][1]
                    ptn = sum(p[2] for p in piece)
                    kr = [seq.kres[p[0]] for p in piece]
                    vr = [seq.vres[p[0]] for p in piece]
                    dma("sp", ktp[:, 0:ptn], seq.kT[h, :, pt0:pt0 + ptn], kr, [ktp])
                    nfull = sum(1 for p in piece if p[2] == 128)
                    if nfull > 0:
                        dma("sp", vp[:, 0:nfull, 0:128],
                            seq.v[pt0:pt0 + nfull * 128, h * 128:(h + 1) * 128].rearrange("(b p) e -> p b e", p=128),
                            vr, [vp])
                    if nfull < len(piece):
                        pl = piece[-1]
                        dma("sp", vp[0:pl[2], nfull, 0:128], seq.v[pl[1]:pl[1] + pl[2], h * 128:(h + 1) * 128], vr, [vp])
                kof = tok0 - kbs[idx - idx % PIECE][1]
                j16 = idx % PIECE
                qa_ = 0 if r is None else r * T
                sset = psS.next()
                for m in range(2):
                    mm(sset[m][0:nk, qa_:QW],
                       [(ktp[m * 64:(m + 1) * 64, kof:kof + nk], QT[m * 64:(m + 1) * 64, h, qa_:QW])],
                       [ktp, QT], [sset[m]])
                p_ = Pt.next()
                act(p_[0:nk, :, qa_:QW], sset[2][0:nk, :].rearrange("p (m q) -> p m q", m=2)[:, :, qa_:QW], AF.Exp,
                    [sset[0], sset[1], bias_res], [p_], bias=bias_ap[0:nk, :], scale=DIFF_SCALE)
                if r is not None:
                    vmemset(p_[64:128, :, r * T:r * T + 64], 0.0, [p_])
                if pend_pv[0] is not None:
                    do_pv(pend_pv[0])
                pend_pv[0] = (p_, vp, j16, r, nk, idx)
                if idx == 4:
                    flush_chain()
                if idx == 28:
                    flush_tails()
            do_pv(pend_pv[0])
            flush_chain()
            flush_tails()
            osbs = []
            for qi in range(nq):
                osb = osbR.next()
                vcopy(osb[0:T, :], psO[qi][0:T, 0:258], [psO[qi]], [osb])
                osbs.append(osb)

            def fin(osbs=osbs, h=h):
                for qi in range(nq):
                    i = qi
                    po = osbs[qi]
                    rr = small.next()
                    ts(rr[0:T, 0:2], po[0:T, 0:258].rearrange("p (m e) -> p m e", m=2)[:, :, 128], 1e-30, None,
                       ALU.add, None, [po], [rr])
                    vrecip(rr[0:T, 0:2], rr[0:T, 0:2], [rr], [rr])
                    tt(rr[0:T, 1:2], rr[0:T, 1:2], nlam[0:T, :], ALU.mult, [rr, nlam], [rr])
                    o0 = oa0.next()
                    ts(o0[0:T, :], po[0:T, 0:128], rr[0:T, 0:1], None, ALU.mult, None, [po, rr], [o0])
                    o1 = oa1.next()
                    stt(o1[0:T, :], po[0:T, 129:257], rr[0:T, 1:2], o0[0:T, :], ALU.mult, ALU.add, [po, rr, o0], [o1])
                    tt(o0[0:T, :], o1[0:T, :], o1[0:T, :], ALU.mult, [o1], [o0])
                    s2 = small.next()
                    vreduce(s2[0:T, 0:1], o0[0:T, :], [o0], [s2])
                    ts(s2[0:T, 0:1], s2[0:T, 0:1], 1.0 / 128, EPS, ALU.mult, ALU.add, [s2], [s2])
                    tt(s2[0:T, 0:1], s2[0:T, 0:1], mhalf_t[0:T, :], ALU.pow, [s2, mhalf_t], [s2], eng="pool")
                    ob_ = oab.next()
                    stt(ob_[0:T, :], o1[0:T, :], s2[0:T, 0:1], sld_t[0:T, :], ALU.mult, ALU.mult, [o1, s2, sld_t], [ob_])

                    def ftail(ob_=ob_, h=h, i=i):
                        pst = psg.next()
                        transpose_to(pst, 0, ob_[0:T, :], ob_, T, 128)
                        vcopy(oaT[:, h, i * T:(i + 1) * T], pst[:, 0:T], [pst], [oaT])
                    fin_tails.append(ftail)
            fin_chain.append(fin)
        flush_chain()
        flush_tails()

    def diff_attention_hb(g, seq, keyblocks_of, oaT):
        T = g.T
        PH = 4
        psg.items = banks[0:4]
        for v_ in Vp.items:
            vmemset(v_[:, :, 128:129], 1.0, [v_])
        kbs = keyblocks_of()
        nkb = len(kbs)
        fin_tails = []
        fin_chain = []

        def flush(lst):
            while lst:
                lst.pop(0)()

        for hg in range(2):
            ktp = vp = None
            pend_pv = [None]

            def do_pv(st):
                p_, vp_, j_, nk_, idx_ = st
                vf_ = vp_[:, :, :].rearrange("p s e -> p (s e)")
                for hs in range(4):
                    for m in range(2):
                        lh = p_[0:nk_, m, hs * T:(hs + 1) * T]
                        mm(psO[hs][0:T, m * 129:m * 129 + 128],
                           [(lh, vf_[0:nk_, j_ * 512 + hs * 128:j_ * 512 + (hs + 1) * 128])], [p_, vp_], [psO[hs]],
                           start=(idx_ == 0 and m == 0), stop=(idx_ == nkb - 1), skipgc=True)
                        mm(psO[hs][0:T, m * 129 + 128:m * 129 + 129], [(lh, ones_b[0:nk_, 0:1])], [p_, ones_b], [psO[hs]],
                           start=False, stop=(idx_ == nkb - 1), skipgc=True)

            for idx, (kbi, tok0, nk, r, bias_ap, bias_res) in enumerate(kbs):
                if idx % PH == 0:
                    piece = kbs[idx:idx + PH]
                    ktp = KTp.next()
                    vp = Vp.next()
                    pt0 = piece[0][1]
                    ptn = sum(p[2] for p in piece)
                    kr = [seq.kres[p[0]] for p in piece]
                    vr = [seq.vres[p[0]] for p in piece]
                    nfull = sum(1 for p in piece if p[2] == 128)
                    dma("sp", ktp[:, :].rearrange("p (h t) -> p h t", h=4)[:, :, 0:ptn],
                        seq.kT[hg * 4:hg * 4 + 4, :, pt0:pt0 + ptn].rearrange("h d t -> d h t"), kr, [ktp])
                    vflat = vp[:, :, :].rearrange("p s e -> p (s e)")
                    if nfull > 0:
                        dma("sp", vflat[:, 0:nfull * 512].rearrange("p (b c) -> p b c", c=512),
                            seq.v[pt0:pt0 + nfull * 128, hg * 512:(hg + 1) * 512].rearrange("(b p) c -> p b c", p=128),
                            vr, [vp])
                    if nfull < len(piece):
                        pl = piece[-1]
                        dma("sp", vflat[0:pl[2], nfull * 512:(nfull + 1) * 512],
                            seq.v[pl[1]:pl[1] + pl[2], hg * 512:(hg + 1) * 512], vr, [vp])
                kof = tok0 - kbs[idx - idx % PH][1]
                j_ = idx % PH
                sset = psS.next()
                for hs in range(4):
                    h = hg * 4 + hs
                    for m in range(2):
                        mm(sset[m][0:nk, hs * T:(hs + 1) * T],
                           [(ktp[m * 64:(m + 1) * 64, hs * 512 + kof:hs * 512 + kof + nk], QT[m * 64:(m + 1) * 64, h, 0:T])],
                           [ktp, QT], [sset[m]])
                p_ = Pt.next()
                act(p_[0:nk, :, 0:4 * T], sset[2][0:nk, :].rearrange("p (m q) -> p m q", m=2)[:, :, 0:4 * T], AF.Exp,
                    [sset[0], sset[1], bias_res], [p_], bias=bias_ap[0:nk, :], scale=DIFF_SCALE)
                if r is not None:
                    vmemset(p_[64:128, :, 0:4 * T].rearrange("p m (h t) -> p m h t", h=4)[:, :, :, 0:64], 0.0, [p_])
                if pend_pv[0] is not None:
                    do_pv(pend_pv[0])
                pend_pv[0] = (p_, vp, j_, nk, idx)
                if idx == 4:
                    flush(fin_chain)
                if idx == 20:
                    flush(fin_tails)
            do_pv(pend_pv[0])
            flush(fin_chain)
            flush(fin_tails)
            osbs = []
            for hs in range(4):
                osb = osbR.next()
                vcopy(osb[0:T, :], psO[hs][0:T, 0:258], [psO[hs]], [osb])
                osbs.append(osb)

            def fin(osbs=osbs, hg=hg):
                for hs in range(4):
                    h = hg * 4 + hs
                    po = osbs[hs]
                    rr = small.next()
                    ts(rr[0:T, 0:2], po[0:T, 0:258].rearrange("p (m e) -> p m e", m=2)[:, :, 128], 1e-30, None,
                       ALU.add, None, [po], [rr])
                    vrecip(rr[0:T, 0:2], rr[0:T, 0:2], [rr], [rr])
                    tt(rr[0:T, 1:2], rr[0:T, 1:2], nlam[0:T, :], ALU.mult, [rr, nlam], [rr])
                    o0 = oa0.next()
                    ts(o0[0:T, :], po[0:T, 0:128], rr[0:T, 0:1], None, ALU.mult, None, [po, rr], [o0])
                    o1 = oa1.next()
                    stt(o1[0:T, :], po[0:T, 129:257], rr[0:T, 1:2], o0[0:T, :], ALU.mult, ALU.add, [po, rr, o0], [o1])
                    tt(o0[0:T, :], o1[0:T, :], o1[0:T, :], ALU.mult, [o1], [o0])
                    s2 = small.next()
                    vreduce(s2[0:T, 0:1], o0[0:T, :], [o0], [s2])
                    ts(s2[0:T, 0:1], s2[0:T, 0:1], 1.0 / 128, EPS, ALU.mult, ALU.add, [s2], [s2])
                    tt(s2[0:T, 0:1], s2[0:T, 0:1], mhalf_t[0:T, :], ALU.pow, [s2, mhalf_t], [s2], eng="pool")
                    ob_ = oab.next()
                    stt(ob_[0:T, :], o1[0:T, :], s2[0:T, 0:1], sld_t[0:T, :], ALU.mult, ALU.mult, [o1, s2, sld_t], [ob_])

                    def ftail(ob_=ob_, h=h):
                        pst = psg.next()
                        transpose_to(pst, 0, ob_[0:T, :], ob_, T, 128)
                        vcopy(oaT[:, h, 0:T], pst[:, 0:T], [pst], [oaT])
                    fin_tails.append(ftail)
            fin_chain.append(fin)
        flush(fin_chain)
        flush(fin_tails)
        psg.items = banks[0:4]

    def mem_stage(g, omT):
        T, NT = g.T, g.NT

        def cons_q(ct, i, ps, ncol):
            qb_ = tb5.next()
            headnorm(ps[0:T, 0:512], ps, T, 512, 256, qnm_t, qb_[0:T, :], qb_)

            def tail():
                ps2 = psg.next()
                for q in range(4):
                    transpose_to(ps2, q * T, qb_[0:T, q * 128:(q + 1) * 128], qb_, T, 128)
                vcopy(QMT[:, ct * 4:ct * 4 + 4, i * T:(i + 1) * T], ps2[:, 0:4 * T].rearrange("p (q t) -> p q t", q=4),
                      [ps2], [QMT])
            return tail
        proj_tok(hT, g, w_in, COL["qm"], 1024, cons_q)
        for h in range(4):
            pm_ = pm.next()
            for mc in range(2):
                ps = psg.next()
                mm(ps[:, 0:NT], [(memKT[:, h, c, mc * 128:(mc + 1) * 128], QMT[:, 2 * h + c, 0:NT]) for c in range(2)],
                   [memKT, QMT], [ps])
                act(pm_[:, mc, 0:NT], ps[:, 0:NT], AF.Exp, [ps], [pm_], scale=MEM_SCALE)
            psl = psg.next()
            mm(psl[:, 0:NT], [(ones_b[:, :], pm_[:, mc, 0:NT]) for mc in range(2)], [ones_b, pm_], [psl])
            rl_ = rl.next()
            vrecip(rl_[:, 0:NT], psl[:, 0:NT], [psl], [rl_])
            for ec in range(2):
                pso = psg.next()
                mm(pso[:, 0:NT], [(memV[:, mc, h, ec * 128:(ec + 1) * 128], pm_[:, mc, 0:NT]) for mc in range(2)],
                   [memV, pm_], [pso])
                tt(omT[:, 2 * h + ec, 0:NT], pso[:, 0:NT], rl_[:, 0:NT], ALU.mult, [pso, rl_], [omT])

    def out_stage(g):
        T = g.T
        mT = oT[1]
        for i in range(g.NB):
            mb_ = tokbf.next()
            acopy(mb_[0:T, :], m_acc[0:T, i, :], [m_acc], [mb_])
            for half in range(2):
                ps = psg.next()
                for q in range(4):
                    kc = half * 4 + q
                    transpose_to(ps, q * T, mb_[0:T, kc * 128:(kc + 1) * 128], mb_, T, 128)
                vcopy(mT[:, half * 4:half * 4 + 4, i * T:(i + 1) * T], ps[:, 0:4 * T].rearrange("p (q t) -> p q t", q=4),
                      [ps], [mT])
        for ct in range(2):
            wb = load_w(w_out, 0, KC, ct * 512, 512)
            for i in range(g.NB):
                ps = psg.next()
                mm(ps[0:T, :], [(mT[:, kc, i * T:(i + 1) * T], wb[:, kc, :]) for kc in range(KC)], [mT, wb], [ps])
                xs_ = x_grp[0:T, i, ct * 512:(ct + 1) * 512]
                tt(xs_, xs_, ps[0:T, :], ALU.add, [x_grp, ps], [x_grp])

    def ffn_stage(g, y_dst, halo):
        T, NT = g.T, g.NT
        carry = cur_carry[0]
        norm_transpose(lambda i: x_grp[0:T, i, :], x_grp, g, gfT, hT)
        for t6 in range(6):
            ncol = 512 if t6 < 5 else 256
            wu = load_w(w_up, 0, KC, t6 * 512, ncol)
            wv = None if halo else load_w(w_up, 0, KC, DFF + t6 * 512, ncol)
            for f4 in range(ncol // 128):
                f = t6 * 4 + f4
                psu = psg.next()
                mm(psu[:, 0:NT], [(wu[:, kc, f4 * 128:(f4 + 1) * 128], hT[:, kc, 0:NT]) for kc in range(KC)],
                   [wu, hT], [psu])
                if halo:
                    ts(carry[:, f, :], psu[:, NT - 2:NT], flag[:, 0:1], None, ALU.mult, None, [psu, flag], [carry])
                    continue
                psv = psg.next()
                mm(psv[:, 0:NT], [(wv[:, kc, f4 * 128:(f4 + 1) * 128], hT[:, kc, 0:NT]) for kc in range(KC)],
                   [wv, hT], [psv])
                u_ = ub.next()
                vcopy(u_[:, 0:2], carry[:, f, :], [carry], [u_])
                acopy(u_[:, 2:2 + NT], psu[:, 0:NT], [psu], [u_])
                t1 = t1b.next()
                ts(t1[:, 0:NT], u_[:, 2:2 + NT], cwT[:, 2, f:f + 1], cbT[:, f:f + 1], ALU.mult, ALU.add,
                   [u_, cwT, cbT], [t1])
                stt(t1[:, 0:NT], u_[:, 1:1 + NT], cwT[:, 1, f:f + 1], t1[:, 0:NT], ALU.mult, ALU.add, [u_, cwT, t1], [t1])
                stt(t1[:, 0:NT], u_[:, 0:NT], cwT[:, 0, f:f + 1], t1[:, 0:NT], ALU.mult, ALU.add, [u_, cwT, t1], [t1])
                act(t1[:, 0:NT], t1[:, 0:NT], AF.Gelu_apprx_tanh, [t1], [t1])
                tt(actT[:, f, 0:NT], t1[:, 0:NT], psv[:, 0:NT], ALU.mult, [t1, psv], [actT])
                vcopy(carry[:, f, :], u_[:, NT:NT + 2], [u_], [carry])
        if halo:
            return
        parts = [(0, 8), (8, 8), (16, 6)]
        for ct in range(2):
            for pi, (f0, nk) in enumerate(parts):
                wb = load_w(w_down, f0 * 128, nk, ct * 512, 512)
                for i in range(g.NB):
                    acc = banks[i]
                    mm(acc[0:T, :], [(actT[:, f0 + k, i * T:(i + 1) * T], wb[:, k, :]) for k in range(nk)],
                       [actT, wb], [acc], start=(pi == 0), stop=(pi == 2))
            for i in range(g.NB):
                yo = f5.next()
                tt(yo[0:T, :], x_grp[0:T, i, ct * 512:(ct + 1) * 512], banks[i][0:T, :], ALU.add, [x_grp, banks[i]], [yo])
                r0 = y_dst[1](i)
                dma("sp", y_dst[0][r0:r0 + T, ct * 512:(ct + 1) * 512], yo[0:T, :], [yo], ())

    xloaded = [False]
    next_load = [None]
    def run_group(g, seq, x_src_of, tok0_of, kres_of, keyblocks_of, outs_kv, y_dst):
        T = g.T
        full = g.kind != "prefix"
        if not xloaded[0]:
            for i in range(g.NB):
                dma("sp", x_grp[0:T, i, :], x_src_of(i), (), [x_grp])
        xloaded[0] = False
        norm_transpose(lambda i: x_grp[0:T, i, :], x_grp, g, gaT, hT)
        if not full and next_load[0] is not None:
            next_load[0]()
            xloaded[0] = True
        ckpt(g.kind + " norm")
        obT = gla_stage(g, full)
        ckpt(g.kind + " gla")
        if full:
            branch_proj(g, obT, w_pg, COL["gb"], True)
            ckpt(g.kind + " proj_gla")
        diff_kv(g, seq, tok0_of, kres_of, outs_kv)
        ckpt(g.kind + " diff_kv")
        if not full:
            return
        diff_q(g)
        ckpt(g.kind + " diff_q")
        oaT = oT[1]
        if g.NB == 1:
            diff_attention_hb(g, seq, keyblocks_of, oaT)
        else:
            diff_attention(g, seq, keyblocks_of, oaT)
        ckpt(g.kind + " diff_attn")
        branch_proj(g, oaT, w_pd, COL["ga"], False)
        omT = oT[0]
        mem_stage(g, omT)
        ckpt(g.kind + " mem")
        branch_proj(g, omT, w_pm, COL["gm"], False)
        out_stage(g)
        ckpt(g.kind + " out")
        ffn_stage(g, y_dst, g.kind == "halo")
        ckpt(g.kind + " ffn")

    stage_no = [0]

    def ckpt(name):
        stage_no[0] += 1
        if STOP_AFTER is not None and stage_no[0] >= STOP_AFTER:
            if not S.stopped:
                print("STOP after stage", stage_no[0], name)
            S.stopped = True

    mem_kv_prompt()
    ckpt("mem_kv")
    sprep_next = [0]

    def sample_prep_some(n):
        while n > 0 and sprep_next[0] < NPB:
            sample_prep_block(sprep_next[0])
            sprep_next[0] += 1
            n -= 1
    vmemset(Sst[:, :, :], 0.0, [Sst])
    vmemset(Sbf[:, :, :], 0.0, [Sbf])
    vmemset(carry_p[:, :, :], 0.0, [carry_p])
    carry_s_next = [0]

    def load_carry_s(n=NFC):
        while n > 0 and carry_s_next[0] < NFC:
            f = carry_s_next[0]
            dma("sp", carry_s[:, f, :], sconv[:, f * 128:(f + 1) * 128].rearrange("t p -> p t"), (), [carry_s], slow=True)
            carry_s_next[0] += 1
            n -= 1

    def prompt_keyblocks(g):
        def f():
            wb0 = g.blocks[0]
            out = []
            for kb in range(0, g.blocks[-1] + 1):
                out.append((kb, kb * 128, 128, None if kb < wb0 else kb - wb0, kbias[:, kb:kb + 1], kbias))
            return out
        return f

    groups = []
    b = 0
    while b < NPRE:
        groups.append(Grp("prefix", list(range(b, min(b + 4, NPRE))), 128))
        b += 4
    groups.append(Grp("halo", [NPRE], 128))
    b = NPRE + 1
    while b < NWIN:
        groups.append(Grp("own", list(range(b, min(b + 4, NWIN))), 128))
        b += 4
    for gi, g in enumerate(groups):
        own0 = NPRE + 1
        conv_on[0] = (g.kind == "prefix")
        next_load[0] = None
        if gi + 1 < len(groups):
            def _nl(g2=groups[gi + 1]):
                for i2 in range(g2.NB):
                    dma("sp", x_grp[0:g2.T, i2, :], xw[g2.blocks[i2] * 128:(g2.blocks[i2] + 1) * 128, :], (), [x_grp])
            next_load[0] = _nl
        outs_kv = None
        y_dst = None
        if g.kind == "own":
            outs_kv = (dk_o, dv_o, (lambda g_: (lambda i: (g_.blocks[i] - own0) * 128))(g))
            y_dst = (y_o, (lambda g_: (lambda i: (g_.blocks[i] - own0) * 128))(g))
        run_group(g, sp_,
                  (lambda g_: (lambda i: xw[g_.blocks[i] * 128:(g_.blocks[i] + 1) * 128, :]))(g),
                  (lambda g_: (lambda i: g_.blocks[i] * 128))(g),
                  (lambda g_: (lambda i: g_.blocks[i]))(g),
                  prompt_keyblocks(g), outs_kv, y_dst)
        if gi >= 1:
            load_carry_s(3)
        if g.kind == "prefix":
            sample_prep_some(2)
            if gi + 1 < len(groups) and groups[gi + 1].kind != "prefix":
                convert_some(4 * NWT)
    sample_prep_some(NPB)
    load_carry_s()
    dma("sp", gs_o.rearrange("(h d) v -> d h v", h=4), Sst[:, :, :], [Sst], ())

    sample_prep_mem()
    dma("sp", Sst[:, :, :], sgla.rearrange("(h d) v -> d h v", h=4), (), [Sst])
    acopy(Sbf[:, :, :], Sst[:, :, :], [Sst], [Sbf])
    cur_carry[0] = carry_s
    gsmp = Grp("sample", [NPB], SD)
    conv_on[0] = False
    next_load[0] = None
    dma("sp", x_grp[0:SD, 0, :], xs[0:SD, :], (), [x_grp])
    xloaded[0] = True
    for f in range(NFC):
        dma("sp", cs_o[:, f * 128:(f + 1) * 128].rearrange("t p -> p t"), carry_p[:, f, :], [carry_p], (), slow=True)

    def sample_keyblocks():
        out = [(kb, kb * 128, 128, None, zero_t[:, 0:1], zero_t) for kb in range(NPB)]
        out.append((NPB, PAST, SD, None, zero_t[:, 0:1], zero_t))
        return out

    run_group(gsmp, ss_, lambda i: xs[0:SD, :], lambda i: PAST, lambda i: NPB, sample_keyblocks,
              (dks_o, dvs_o, lambda i: 0), (ys_o, lambda i: 0))
    dma("sp", gss_o.rearrange("(h d) v -> d h v", h=4), Sst[:, :, :], [Sst], ())
    for f in range(NFC):
        dma("sp", css_o[:, f * 128:(f + 1) * 128].rearrange("t p -> p t"), carry_s[:, f, :], [carry_s], (), slow=True)

    S.emit()
    print("ops per engine:", S.stats, "waits:", S.nwaits)
    return nc


_CACHE = {}


def _prep_inputs(inp):
    xp = np.asarray(inp["x_prompt"], np.float32)
    B, SEQ, _ = xp.shape
    xs = np.asarray(inp["x_sample"], np.float32)
    DB, SD, _ = xs.shape
    PAST = inp["cache_diff_k"].shape[2]
    NWIN = SEQ // 128
    OWN = SEQ // 4
    ident = np.eye(128, dtype=np.float32)
    tri = np.triu(np.ones((128, 128), np.float32))
    shared = {"ident": ident, "tri": tri}
    for k in ("g_attn", "w_in", "w_gk2", "b_gk", "qn_diff", "kn_diff", "lam_q1", "lam_k1", "lam_q2", "lam_k2",
              "subln_diff", "subln_gla", "g_mem", "w_mem_kv", "qn_mem", "kn_mem", "w_proj_diff", "w_proj_gla",
              "w_proj_mem", "w_out", "g_ffn", "w_up", "conv_w", "conv_b", "w_down"):
        a = np.asarray(inp[k], np.float32)[0]
        if a.ndim == 1:
            a = a[None, :]
        shared[k] = np.ascontiguousarray(a)
    maps = []
    for c in range(8):
        n, j = c // 4, c % 4
        end = OWN * (j + 1)
        start = end - SEQ
        xw = np.zeros((SEQ, D), np.float32)
        xw[max(0, -start):] = xp[n, max(0, start):end]
        kb = np.zeros((128, NWIN), np.float32)
        npad = max(0, -start) // 128
        kb[:, :npad] = NEG
        m = dict(shared)
        m["xw"] = xw
        m["kbias"] = kb
        m["flag"] = np.full((128, 1), 1.0 if j > 0 else 0.0, np.float32)
        m["xs"] = np.ascontiguousarray(xs[c])
        m["ck"] = np.ascontiguousarray(np.asarray(inp["cache_diff_k"], np.float32)[0, c].reshape(PAST, D))
        m["cv"] = np.ascontiguousarray(np.asarray(inp["cache_diff_v"], np.float32)[0, c].reshape(PAST, D))
        m["cmk"] = np.ascontiguousarray(np.asarray(inp["cache_mem_k"], np.float32)[0, c].reshape(256, D))
        m["cmv"] = np.ascontiguousarray(np.asarray(inp["cache_mem_v"], np.float32)[0, c].reshape(256, D))
        m["sgla"] = np.ascontiguousarray(np.asarray(inp["state_gla"], np.float32)[0, c].reshape(512, 256))
        m["sconv"] = np.ascontiguousarray(np.asarray(inp["state_conv"], np.float32)[0, c])
        m["mem"] = np.ascontiguousarray(np.asarray(inp["mem_prompt"], np.float32)[n])
        maps.append(m)
    return maps, (B, SEQ, DB, SD, PAST, NWIN, OWN)


def kernel(**inp):
    maps, (B, SEQ, DB, SD, PAST, NWIN, OWN) = _prep_inputs(inp)
    key = (NWIN, PAST, SD)
    if key not in _CACHE:
        _CACHE[key] = build(NWIN, PAST, SD)
    nc = _CACHE[key]
    res = run_bass_kernel_spmd(nc, maps, core_ids=list(range(8)))
    R = res.results
    y = np.zeros((B, SEQ, D), np.float32)
    dk = np.zeros((1, B, SEQ, 8, 2, 64), np.float32)
    dv = np.zeros((1, B, SEQ, 8, 128), np.float32)
    mk = np.zeros((1, B, 256, 4, 256), np.float32)
    mv = np.zeros((1, B, 256, 4, 256), np.float32)
    gs = np.zeros((1, B, 4, 128, 256), np.float32)
    cs = np.zeros((1, B, 2, DFF), np.float32)
    ys = np.zeros((DB, SD, D), np.float32)
    dks = np.zeros((1, DB, SD, 8, 2, 64), np.float32)
    dvs = np.zeros((1, DB, SD, 8, 128), np.float32)
    gss = np.zeros((1, DB, 4, 128, 256), np.float32)
    css = np.zeros((1, DB, 2, DFF), np.float32)
    for c in range(8):
        n, j = c // 4, c % 4
        r = R[c]
        sl = slice(OWN * j, OWN * (j + 1))
        y[n, sl] = r["y"]
        dk[0, n, sl] = r["dk"].reshape(OWN, 8, 2, 64)
        dv[0, n, sl] = r["dv"].reshape(OWN, 8, 128)
        if j == 0:
            mk[0, n] = r["mk"].reshape(256, 4, 256)
            mv[0, n] = r["mv"].reshape(256, 4, 256)
        if j == 3:
            gs[0, n] = r["gs"].reshape(4, 128, 256)
            cs[0, n] = r["cs"]
        ys[c] = r["ys"]
        dks[0, c] = r["dks"].reshape(SD, 8, 2, 64)
        dvs[0, c] = r["dvs"].reshape(SD, 8, 128)
        gss[0, c] = r["gss"].reshape(4, 128, 256)
        css[0, c] = r["css"]
    return (y, ys, dk, dv, mk, mv, gs, cs, dks, dvs, gss, css)
```

```python
import numpy as np
import concourse.bass as bass
import concourse.mybir as mybir
from concourse.bass_utils import run_bass_kernel_spmd

F32 = mybir.dt.float32
BF16 = mybir.dt.bfloat16
AF = mybir.ActivationFunctionType
ALU = mybir.AluOpType
AX = mybir.AxisListType

D = 1024
KC = 8
DFF = 2816
NFC = 22
EPS = 1e-6
COL = dict(qa=0, ka=1024, va=2048, qb=3072, kb=3584, vb=4096, rb=5120, gk=6144, qm=6160, ga=7184, gb=8208, gm=9232)
DIFF_SCALE = 64 ** -0.5
GLA_SCALE = 128 ** -0.5
MEM_SCALE = 256 ** -0.5
LAM_INIT = 0.2
NEG = -30000.0
PIECE = 16
FOLD_WAIT = True
FOLD_DMA = True
CONV_PER_GROUP = 4
STOP_AFTER = None
DBG = 9


class _Stop(Exception):
    pass


class Res:
    __slots__ = ("name", "w", "r", "t", "off", "size", "al", "excl")

    def __init__(self, name, t=None, off=None, size=None):
        self.name = name
        self.w = None
        self.r = []
        self.t = t
        self.off = off
        self.size = size
        self.al = []
        self.excl = False

    def __getitem__(self, k):
        return self.t[k]


class Op:
    __slots__ = ("eng", "fn", "deps", "signal", "event", "prewait", "dma", "single")

    def __init__(self, eng, fn, dma):
        self.eng = eng
        self.fn = fn
        self.deps = []
        self.signal = dma
        self.event = None
        self.prewait = None
        self.dma = dma
        self.single = True


class Sched:
    ENGS = ("pe", "act", "dve", "pool", "sp")

    def __init__(self, nc, n_dma_sems=32):
        self.nc = nc
        self.ops = []
        self.stopped = False
        self.n_dma_sems = n_dma_sems

    def op(self, eng, fn, reads=(), writes=(), dma=False):
        if self.stopped:
            return None
        o = Op(eng, fn, dma)
        deps = []
        for t in reads:
            if t.w is not None:
                deps.append(t.w)
            if t.excl:
                deps.extend(r_ for r_ in t.r if r_.eng != eng)
        for t in writes:
            if t.w is not None:
                deps.append(t.w)
            deps.extend(t.r)
            for a in t.al:
                if a.w is not None:
                    deps.append(a.w)
                deps.extend(a.r)
        seen = set()
        for d in deps:
            if id(d) in seen:
                continue
            seen.add(id(d))
            if d.eng == "pe" and eng == "pe" and not dma and not d.dma:
                continue
            o.deps.append(d)
            d.signal = True
        for t in reads:
            t.r.append(o)
        for t in writes:
            t.w = o
            t.r = []
        self.ops.append(o)
        return o

    def emit(self):
        nc = self.nc
        esem = {e: nc.alloc_semaphore("s_" + e) for e in self.ENGS}
        dsem = [nc.alloc_semaphore("d%d" % i) for i in range(self.n_dma_sems)]
        dtot = [0] * self.n_dma_sems
        ecount = {e: 0 for e in self.ENGS}
        n_sw = 8
        nd = {"pool": 0, "sp": 0}
        for o in self.ops:
            if o.dma:
                if o.eng == "pool":
                    k = nd["pool"] % n_sw
                else:
                    k = n_sw + nd[o.eng] % (self.n_dma_sems - n_sw)
                nd[o.eng] += 1
                o.prewait = (dsem[k], dtot[k], ("d", k))
                dtot[k] += 16
                o.event = (dsem[k], dtot[k], ("d", k))
            elif o.signal:
                ecount[o.eng] += 1
                o.event = (esem[o.eng], ecount[o.eng], ("e", o.eng))
        self.stats = {e: 0 for e in self.ENGS}
        self.nwaits = {e: 0 for e in self.ENGS}
        per = {e: [o for o in self.ops if o.eng == e] for e in self.ENGS}
        with nc.Block() as block:
            def run(ename):
                def body(eng):
                    waited = {}
                    for o in per[ename]:
                        ws = [d.event for d in o.deps]
                        if o.prewait is not None and o.prewait[1] > 0:
                            ws.append(o.prewait)
                        need = {}
                        for (sem, val, key) in ws:
                            if waited.get(key, 0) >= val:
                                continue
                            if need.get(key, (None, 0))[1] < val:
                                need[key] = (sem, val)
                        items = list(need.items())
                        fold = None
                        if FOLD_WAIT and items and o.single and (FOLD_DMA or (not o.dma and ename in ("pe", "act", "dve"))):
                            pick = len(items) - 1
                            for ii_, (k_, _) in enumerate(items):
                                if k_ != ("e", ename):
                                    pick = ii_
                            fold = items.pop(pick)
                        for key, (sem, val) in items:
                            eng.wait_ge(sem, val)
                            waited[key] = val
                            self.nwaits[ename] += 1
                        ins = o.fn(eng)
                        if fold is not None:
                            ins._wait_ge(fold[1][0], fold[1][1])
                            waited[fold[0]] = fold[1][1]
                        self.stats[ename] += 1
                        if o.dma:
                            ins.then_inc(o.event[0], 16)
                        elif o.signal:
                            ins.then_inc(o.event[0], 1)
                    last = {}
                    for o in per[ename]:
                        if o.dma:
                            last[o.event[2]] = o.event
                    for key, (sem, val, _) in last.items():
                        if waited.get(key, 0) < val:
                            eng.wait_ge(sem, val)
                return body
            block.tensor(run("pe"))
            block.scalar(run("act"))
            block.vector(run("dve"))
            block.gpsimd(run("pool"))
            block.sync(run("sp"))


class Rot:
    def __init__(self, items):
        self.items = items
        self.i = 0

    def next(self):
        r = self.items[self.i % len(self.items)]
        self.i += 1
        return r


class Grp:
    def __init__(self, kind, blocks, T):
        self.kind = kind
        self.blocks = blocks
        self.T = T
        self.NB = len(blocks)
        self.NT = self.NB * T


def build(NWIN, PAST, SD=64):
    NOWN = NWIN // 4
    NPRE = NWIN - NOWN - 1
    NTOKW = NWIN * 128
    NPB = PAST // 128
    nc = bass.Bass("TRN2", target_bir_lowering=False)
    S = Sched(nc)

    def din(name, shape):
        return nc.dram_tensor(name, shape, F32, kind="ExternalInput").ap()

    def dout(name, shape):
        return nc.dram_tensor(name, shape, F32, kind="ExternalOutput").ap()

    xw = din("xw", [NTOKW, D])
    kbias_d = din("kbias", [128, NWIN])
    flag_d = din("flag", [128, 1])
    xs = din("xs", [SD, D])
    ck = din("ck", [PAST, D])
    cv = din("cv", [PAST, D])
    cmk = din("cmk", [256, D])
    cmv = din("cmv", [256, D])
    sgla = din("sgla", [512, 256])
    sconv = din("sconv", [2, DFF])
    memx = din("mem", [256, D])
    ident_d = din("ident", [128, 128])
    tri_d = din("tri", [128, 128])
    g_attn = din("g_attn", [1, D])
    w_in = din("w_in", [D, 10256])
    w_gk2 = din("w_gk2", [16, 512])
    b_gk = din("b_gk", [1, 512])
    qn_diff = din("qn_diff", [1, 64])
    kn_diff = din("kn_diff", [1, 64])
    lam_d = [din(n, [1, 64]) for n in ("lam_q1", "lam_k1", "lam_q2", "lam_k2")]
    subln_diff = din("subln_diff", [1, 128])
    subln_gla = din("subln_gla", [1, 256])
    g_mem = din("g_mem", [1, D])
    w_mem_kv = din("w_mem_kv", [D, 2048])
    qn_mem = din("qn_mem", [1, 256])
    kn_mem = din("kn_mem", [1, 256])
    w_pd = din("w_proj_diff", [D, D])
    w_pg = din("w_proj_gla", [D, D])
    w_pm = din("w_proj_mem", [D, D])
    w_out = din("w_out", [D, D])
    g_ffn = din("g_ffn", [1, D])
    w_up = din("w_up", [D, 2 * DFF])
    conv_w = din("conv_w", [3, DFF])
    conv_b = din("conv_b", [1, DFF])
    w_down = din("w_down", [DFF, D])

    y_o = dout("y", [NOWN * 128, D])
    ys_o = dout("ys", [SD, D])
    dk_o = dout("dk", [NOWN * 128, D])
    dv_o = dout("dv", [NOWN * 128, D])
    mk_o = dout("mk", [256, D])
    mv_o = dout("mv", [256, D])
    gs_o = dout("gs", [512, 256])
    cs_o = dout("cs", [2, DFF])
    dks_o = dout("dks", [SD, D])
    dvs_o = dout("dvs", [SD, D])
    gss_o = dout("gss", [512, 256])
    css_o = dout("css", [2, DFF])

    NTOKS = PAST + 128
    kT_p = nc.dram_tensor("kT_p", [8, 128, NTOKW], BF16).ap()
    v_p = nc.dram_tensor("v_p", [NTOKW, 1024], BF16).ap()
    kT_s = nc.dram_tensor("kT_s", [8, 128, NTOKS], BF16).ap()
    v_s = nc.dram_tensor("v_s", [NTOKS, 1024], BF16).ap()

    NWT = 48
    wscr = nc.dram_tensor("wscr", [NWT, 128, KC * 512], BF16).ap()
    wres = [Res("wres%d" % i) for i in range(NWT)]
    wslot = {}

    sb_lo, sb_hi = nc.bump_sbuf(207 * 1024)
    cur = [sb_lo]
    allres = []

    def esz(dt):
        return 4 if dt == F32 else 2

    def tile(name, shape, dt=F32, at=None):
        n = 1
        for s in shape[1:]:
            n *= s
        size = (n * esz(dt) + 31) // 32 * 32
        if at is None:
            off = cur[0]
            cur[0] += size
        else:
            off = at[0]
            at[0] += size
        assert off + size <= sb_hi, ("SBUF overflow", name, off + size - sb_hi)
        t = nc.alloc_sbuf_tensor_at(name, shape, dt, offset=off)
        r = Res(name, t, off, size)
        allres.append(r)
        return r

    ident_f = tile("ident_f", [128, 128])
    identb = tile("identb", [128, 128], BF16)
    tri_f = tile("tri_f", [128, 128])
    ones_f = tile("ones_f", [128, 128])
    trim_f = tile("trim_f", [128, 128])
    ones_b = tile("ones_b", [128, 128], BF16)
    eps_t = tile("eps_t", [128, 1])
    one_t = tile("one_t", [128, 1])
    zero_t = tile("zero_t", [128, 1])
    mhalf_t = tile("mhalf_t", [128, 1])
    gaT = tile("gaT", [128, 8])
    gfT = tile("gfT", [128, 8])
    gmT = tile("gmT", [128, 8])
    qn_t = tile("qn_t", [128, 64])
    kn_t = tile("kn_t", [128, 64])
    sld_t = tile("sld_t", [128, 128])
    slg_t = tile("slg_t", [128, 256])
    qnm_t = tile("qnm_t", [128, 256])
    knm_t = tile("knm_t", [128, 256])
    bgk_t = tile("bgk_t", [128, 512])
    wgk2_t = tile("wgk2_t", [16, 512])
    cwT = tile("cwT", [128, 3, NFC])
    cbT = tile("cbT", [128, NFC])
    lam_t = [tile("lam%d" % i, [128, 64]) for i in range(4)]
    lamtmp = tile("lamtmp", [128, 64])
    lame = tile("lame", [128, 2])
    nlam = tile("nlam", [128, 1])
    kbias = tile("kbias", [128, NWIN])
    flag = tile("flag", [128, 1])
    carry_p = tile("carry", [128, NFC, 2])
    carry_s = tile("carry_s", [128, NFC, 2])
    cur_carry = [carry_p]
    Sst = tile("Sst", [128, 4, 256])
    Sbf = tile("Sbf", [128, 4, 256], BF16)
    memKT = tile("memKT", [128, 4, 2, 256], BF16)
    memV = tile("memV", [128, 2, 4, 256], BF16)
    x_grp = tile("x_grp", [128, 4, D])
    hT = tile("hT", [128, KC, 512], BF16)
    m_acc = tile("m_acc", [128, 4, D])
    oT = [tile("oT%d" % i, [128, KC, 512], BF16) for i in range(2)]
    wbufs = Rot([tile("wb%d" % i, [128, KC, 512], BF16) for i in range(3)])
    sqb = Rot([tile("sq%d" % i, [128, 1024], BF16) for i in range(1)])
    sq5 = Rot([tile("sq5_%d" % i, [128, 512]) for i in range(2)])
    nrm5 = Rot([tile("nrm5_%d" % i, [128, 512]) for i in range(2)])
    f5 = Rot([tile("f5_%d" % i, [128, 512]) for i in range(3)])
    tokbf = Rot([tile("tokbf%d" % i, [128, D], BF16) for i in range(4)])
    tb5 = Rot([tile("tb5_%d" % i, [128, 512], BF16) for i in range(3)])
    kTblk = Rot([tile("kTblk%d" % i, [128, 4, 128], BF16) for i in range(2)])
    small = Rot([tile("small%d" % i, [128, 16]) for i in range(8)])
    gkT = tile("gkT", [16, 512])
    cvt_t = tile("cvt_t", [128, KC, 256], BF16)
    arena0 = cur[0]

    at = [arena0]
    Lst = tile("Lst", [128, 4, 512], at=at)
    Kst = tile("Kst", [128, 4, 512], at=at)
    Qst = tile("Qst", [128, 4, 512], at=at)
    Vst = tile("Vst", [128, 4, 1024], BF16, at=at)
    Rst = tile("Rst", [128, 4, 1024], at=at)
    gexpR = Rot([tile("gexp%d" % i, [128, 512], at=at) for i in range(4)])
    gdecR = Rot([tile("gdec%d" % i, [128, 4], at=at) for i in range(2)])
    gq = tile("gq", [128, 512], BF16, at=at)
    gkt = tile("gkt", [128, 512], BF16, at=at)
    gkhR = Rot([tile("gkh%d" % i, [128, 512], BF16, at=at) for i in range(2)])
    gqT = tile("gqT", [128, 4, 128], BF16, at=at)
    gkT2 = tile("gkT2", [128, 4, 128], BF16, at=at)
    gatt = tile("gatt", [128, 4, 128], BF16, at=at)
    end_gla = at[0]
    at = [arena0]
    QT = tile("QT", [128, 8, 512], BF16, at=at)
    KTp = Rot([tile("KTp%d" % i, [128, 2048], BF16, at=at) for i in range(3)])
    Vp = Rot([tile("Vp%d" % i, [128, 16, 129], BF16, at=at) for i in range(3)])
    Pt = Rot([tile("Pt%d" % i, [128, 2, 512], BF16, at=at) for i in range(3)])
    oa0 = Rot([tile("oa0_%d" % i, [128, 128], at=at) for i in range(2)])
    oa1 = Rot([tile("oa1_%d" % i, [128, 128], at=at) for i in range(2)])
    oab = Rot([tile("oab_%d" % i, [128, 128], BF16, at=at) for i in range(8)])
    osbR = Rot([tile("osb_%d" % i, [128, 258], at=at) for i in range(8)])
    end_att = at[0]
    at = [arena0]
    QMT = tile("QMT", [128, 8, 512], BF16, at=at)
    pm = Rot([tile("pm%d" % i, [128, 2, 512], BF16, at=at) for i in range(2)])
    rl = Rot([tile("rl%d" % i, [128, 512], at=at) for i in range(2)])
    end_mem = at[0]
    at = [arena0]
    actT = tile("actT", [128, NFC, 512], BF16, at=at)
    ub = Rot([tile("ub%d" % i, [128, 516], at=at) for i in range(2)])
    t1b = Rot([tile("t1b%d" % i, [128, 512], at=at) for i in range(2)])
    end_ffn = at[0]
    print("SBUF: persistent %d, arena gla %d att %d mem %d ffn %d, limit %d" % (
        arena0 - sb_lo, end_gla - arena0, end_att - arena0, end_mem - arena0, end_ffn - arena0, sb_hi - arena0))

    for i, a in enumerate(allres):
        for b in allres[i + 1:]:
            if a.off < b.off + b.size and b.off < a.off + a.size:
                a.al.append(b)
                b.al.append(a)

    dbl = [nc.alloc_psum_tensor("dbank%d" % i, [128, 1024], F32) for i in range(4)]
    banks = [Res("bank%d" % i, dbl[i // 2][:, (i % 2) * 512:(i % 2 + 1) * 512]) for i in range(8)]
    for b_ in banks:
        b_.excl = True
    psg = Rot(banks[0:4])
    psS = Rot([(banks[0], banks[1], dbl[0]), (banks[2], banks[3], dbl[1])])
    psO = banks[4:8]

    def act(out, in_, func, r, w, bias=None, scale=1.0, accum=None):
        kw = {}
        if bias is not None:
            kw["bias"] = bias
        if accum is not None:
            kw["accum_out"] = accum
        S.op("act", lambda e: e.activation(out=out, in_=in_, func=func, scale=scale, **kw), r, w)

    def tt(out, in0, in1, op, r, w, eng="dve"):
        S.op(eng, lambda e: e.tensor_tensor(out=out, in0=in0, in1=in1, op=op), r, w)

    def ts(out, in0, s1, s2, op0, op1, r, w):
        if s2 is None:
            S.op("dve", lambda e: e.tensor_scalar(out=out, in0=in0, scalar1=s1, scalar2=None, op0=op0), r, w)
        else:
            S.op("dve", lambda e: e.tensor_scalar(out=out, in0=in0, scalar1=s1, scalar2=s2, op0=op0, op1=op1), r, w)

    def stt(out, in0, scalar, in1, op0, op1, r, w):
        S.op("dve", lambda e: e.scalar_tensor_tensor(out=out, in0=in0, scalar=scalar, in1=in1, op0=op0, op1=op1), r, w)

    def vcopy(out, in_, r, w):
        S.op("dve", lambda e: e.tensor_copy(out=out, in_=in_), r, w)

    def acopy(out, in_, r, w):
        S.op("act", lambda e: e.copy(out=out, in_=in_), r, w)

    def vreduce(out, in_, r, w):
        S.op("dve", lambda e: e.tensor_reduce(out=out, in_=in_, axis=AX.X, op=ALU.add), r, w)

    def vrecip(out, in_, r, w):
        S.op("dve", lambda e: e.reciprocal(out=out, in_=in_), r, w)

    def vmemset(ap, val, w):
        S.op("dve", lambda e: e.memset(ap, val), (), w)

    def mm(out, pairs, r, w, start=True, stop=True, skipgc=False):
        def fn(e):
            n = len(pairs)
            ins = None
            for i, (l, rh) in enumerate(pairs):
                if skipgc:
                    ins = e.matmul(out, lhsT=l, rhs=rh, start=(start and i == 0), stop=(stop and i == n - 1),
                                   skip_group_check=True)
                else:
                    ins = e.matmul(out, lhsT=l, rhs=rh, start=(start and i == 0), stop=(stop and i == n - 1))
            return ins
        o_ = S.op("pe", fn, r, w)
        if o_ is not None:
            o_.single = (len(pairs) == 1)

    def dma(eng, out, in_, r, w, slow=False):
        if slow:
            S.op(eng, lambda e: e.dma_start(out=out, in_=in_, allow_slow_non_contiguous=True), r, w, dma=True)
        else:
            S.op(eng, lambda e: e.dma_start(out=out, in_=in_), r, w, dma=True)

    def g3(ap, g):
        return ap.rearrange("p (g d) -> p g d", g=g)

    conv_on = [False]

    def load_w(w2d, r0, nk, c0, ncols):
        wb = load_w_(w2d, r0, nk, c0, ncols)
        if conv_on[0]:
            convert_some(1)
        return wb

    def load_w_(w2d, r0, nk, c0, ncols):
        wb = wbufs.next()
        key = (id(w2d), r0, nk, c0, ncols)
        if key in wslot:
            sl = wslot[key]
            dma("pool", wb[:, 0:nk, 0:ncols],
                wscr[sl, :, 0:nk * ncols].rearrange("p (k c) -> p k c", k=nk), [wres[sl]], [wb])
        else:
            dma("pool", wb[:, 0:nk, 0:ncols],
                w2d[r0:r0 + nk * 128, c0:c0 + ncols].rearrange("(k p) c -> p k c", p=128), (), [wb])
        return wb

    def weight_keys():
        ks = []

        def k(w, r0, nk, c0, n_):
            ks.append((w, r0, nk, c0, n_))
        for c_ in (COL["gk"],):
            k(w_in, 0, KC, c_, 16)
        for nm in ("kb", "vb", "ka", "va"):
            k(w_in, 0, KC, COL[nm], 512)
            if nm != "kb":
                k(w_in, 0, KC, COL[nm] + 512, 512)
        k(w_in, 0, KC, COL["qb"], 512)
        k(w_in, 0, KC, COL["rb"], 512)
        k(w_in, 0, KC, COL["rb"] + 512, 512)
        for (gname, wp) in (("gb", w_pg), ("ga", w_pd), ("gm", w_pm)):
            if gname == "ga":
                k(w_in, 0, KC, COL["qa"], 512)
                k(w_in, 0, KC, COL["qa"] + 512, 512)
            if gname == "gm":
                k(w_in, 0, KC, COL["qm"], 512)
                k(w_in, 0, KC, COL["qm"] + 512, 512)
            for ct in range(2):
                k(w_in, 0, KC, COL[gname] + ct * 512, 512)
                k(wp, 0, KC, ct * 512, 512)
        for ct in range(2):
            k(w_out, 0, KC, ct * 512, 512)
        for t6 in range(6):
            ncol = 512 if t6 < 5 else 256
            k(w_up, 0, KC, t6 * 512, ncol)
            k(w_up, 0, KC, DFF + t6 * 512, ncol)
        for ct in range(2):
            for (f0, nk) in ((0, 8), (8, 8), (16, 6)):
                k(w_down, f0 * 128, nk, ct * 512, 512)
        return ks

    wk_pending = weight_keys()
    assert len(wk_pending) <= NWT, len(wk_pending)

    cv_state = {"slots": 0, "half": 0}

    def convert_some(n):
        while n > 0 and wk_pending:
            (w2d, r0, nk, c0, ncols) = wk_pending[0]
            sl = cv_state["slots"]
            h0 = cv_state["half"] * 256
            hn = min(256, ncols - h0)
            dma("pool", cvt_t[:, 0:nk, 0:hn],
                w2d[r0:r0 + nk * 128, c0 + h0:c0 + h0 + hn].rearrange("(k p) c -> p k c", p=128), (), [cvt_t])
            dma("sp", wscr[sl, :, 0:nk * ncols].rearrange("p (k c) -> p k c", k=nk)[:, :, h0:h0 + hn],
                cvt_t[:, 0:nk, 0:hn], [cvt_t], [wres[sl]])
            if h0 + hn >= ncols:
                wk_pending.pop(0)
                wslot[(id(w2d), r0, nk, c0, ncols)] = sl
                cv_state["slots"] += 1
                cv_state["half"] = 0
            else:
                cv_state["half"] += 1
            n -= 1

    def transpose_to(ps, col0, src_ap, src_res, Tn, ncol):
        mm(ps[0:ncol, col0:col0 + Tn], [(src_ap, identb[0:Tn, 0:Tn])], [src_res, identb], [ps])

    def headnorm(src, src_res, T, n, gs, gain, dst, dst_res, extra=None):
        ng = n // gs
        sq = sq5.next()
        act(sq[0:T, 0:n], src, AF.Square, [src_res], [sq])
        ss = small.next()
        vreduce(ss[0:T, 0:ng], g3(sq[0:T, 0:n], ng), [sq], [ss])
        act(ss[0:T, 0:ng], ss[0:T, 0:ng], AF.Ln, [ss, eps_t], [ss], bias=eps_t[0:T, :], scale=1.0 / gs)
        act(ss[0:T, 0:ng], ss[0:T, 0:ng], AF.Exp, [ss], [ss], scale=-0.5)
        nr = nrm5.next()
        tt(g3(nr[0:T, 0:n], ng), g3(src, ng), ss[0:T, 0:ng].unsqueeze(2).to_broadcast([T, ng, gs]), ALU.mult,
           [src_res, ss], [nr])
        gb = gain[0:T, 0:gs].unsqueeze(1).to_broadcast([T, ng, gs])
        if extra is None:
            tt(g3(dst, ng), g3(nr[0:T, 0:n], ng), gb, ALU.mult, [nr, gain], [dst_res])
        else:
            ex_ap, ex_res = extra
            tt(g3(nr[0:T, 0:n], ng), g3(nr[0:T, 0:n], ng), gb, ALU.mult, [nr, gain], [nr])
            tt(dst, nr[0:T, 0:n], ex_ap, ALU.mult, [nr, ex_res], [dst_res])

    def norm_transpose(src_ap_fn, src_res, g, gT_t, dst):
        T = g.T
        xbs = []
        for i in range(g.NB):
            src = src_ap_fn(i)
            sq = sqb.next()
            ss = small.next()
            act(sq[0:T, :], src, AF.Square, [src_res], [sq, ss], accum=ss[0:T, 0:1])
            act(ss[0:T, 0:1], ss[0:T, 0:1], AF.Ln, [ss, eps_t], [ss], bias=eps_t[0:T, :], scale=1.0 / D)
            act(ss[0:T, 0:1], ss[0:T, 0:1], AF.Exp, [ss], [ss], scale=-0.5)
            xb = tokbf.next()
            ts(xb[0:T, :], src, ss[0:T, 0:1], None, ALU.mult, None, [src_res, ss], [xb])
            xbs.append(xb)
        for i in range(g.NB):
            xb = xbs[i]
            for half in range(2):
                ps = psg.next()
                for q in range(4):
                    kc = half * 4 + q
                    transpose_to(ps, q * T, xb[0:T, kc * 128:(kc + 1) * 128], xb, T, 128)
                tt(dst[:, half * 4:half * 4 + 4, i * T:(i + 1) * T],
                   ps[:, 0:4 * T].rearrange("p (q t) -> p q t", q=4),
                   gT_t[:, half * 4:half * 4 + 4].unsqueeze(2).to_broadcast([128, 4, T]), ALU.mult,
                   [ps, gT_t], [dst])

    def proj_tok(hsrc, g, w2d, c0, ntot, consumer):
        T = g.T
        ct = 0
        c = 0
        pend = [None]
        while c < ntot:
            ncol = min(512, ntot - c)
            wb = load_w(w2d, 0, KC, c0 + c, ncol)
            for i in range(g.NB):
                ps = psg.next()
                mm(ps[0:T, 0:ncol],
                   [(hsrc[:, kc, i * T:(i + 1) * T], wb[:, kc, 0:ncol]) for kc in range(KC)],
                   [hsrc, wb], [ps])
                tail = consumer(ct, i, ps, ncol)
                if pend[0] is not None:
                    pend[0]()
                pend[0] = tail
            c += ncol
            ct += 1
        if pend[0] is not None:
            pend[0]()

    dma("sp", ident_f[:, :], ident_d[:, :], (), [ident_f])
    dma("sp", tri_f[:, :], tri_d[:, :], (), [tri_f])
    vcopy(identb[:, :], ident_f[:, :], [ident_f], [identb])
    ts(trim_f[:, :], tri_f[:, :], -1.0, None, ALU.add, None, [tri_f], [trim_f])
    vmemset(ones_f[:, :], 1.0, [ones_f])
    vmemset(ones_b[:, :], 1.0, [ones_b])
    vmemset(eps_t[:, :], EPS, [eps_t])
    vmemset(one_t[:, :], 1.0, [one_t])
    vmemset(zero_t[:, :], 0.0, [zero_t])
    vmemset(mhalf_t[:, :], -0.5, [mhalf_t])
    for (t_, d_) in ((gaT, g_attn), (gfT, g_ffn), (gmT, g_mem)):
        dma("sp", t_[:, :], d_[0, :].rearrange("(k p) -> p k", p=128), (), [t_], slow=True)
    for (t_, d_, n_) in ((qn_t, qn_diff, 64), (kn_t, kn_diff, 64), (sld_t, subln_diff, 128), (slg_t, subln_gla, 256),
                         (qnm_t, qn_mem, 256), (knm_t, kn_mem, 256), (bgk_t, b_gk, 512)):
        dma("sp", t_[:, :], d_[0:1, :].partition_broadcast(128), (), [t_])
    for i in range(4):
        dma("sp", lam_t[i][:, :], lam_d[i][0:1, :].partition_broadcast(128), (), [lam_t[i]])
    dma("sp", wgk2_t[:, :], w_gk2[:, :], (), [wgk2_t])
    for j in range(3):
        dma("sp", cwT[:, j, :], conv_w[j, :].rearrange("(c p) -> p c", p=128), (), [cwT], slow=True)
    dma("sp", cbT[:, :], conv_b[0, :].rearrange("(c p) -> p c", p=128), (), [cbT], slow=True)
    dma("sp", kbias[:, :], kbias_d[:, :], (), [kbias])
    dma("sp", flag[:, :], flag_d[:, :], (), [flag])
    for i in range(2):
        tt(lamtmp[:, :], lam_t[2 * i][:, :], lam_t[2 * i + 1][:, :], ALU.mult, [lam_t[2 * i], lam_t[2 * i + 1]], [lamtmp])
        vreduce(lame[:, i:i + 1], lamtmp[:, :], [lamtmp], [lame])
    act(lame[:, :], lame[:, :], AF.Exp, [lame], [lame])
    tt(nlam[:, :], lame[:, 1:2], lame[:, 0:1], ALU.subtract, [lame], [nlam])
    ts(nlam[:, :], nlam[:, :], -LAM_INIT, None, ALU.add, None, [nlam], [nlam])
    ts(sld_t[:, :], sld_t[:, :], 1.0 - LAM_INIT, None, ALU.mult, None, [sld_t], [sld_t])
    class Seq:
        pass

    sp_ = Seq()
    sp_.kT, sp_.v = kT_p, v_p
    sp_.kres = [Res("kres_p%d" % i) for i in range(NWIN)]
    sp_.vres = [Res("vres_p%d" % i) for i in range(NWIN)]
    ss_ = Seq()
    ss_.kT, ss_.v = kT_s, v_s
    ss_.kres = [Res("kres_s%d" % i) for i in range(NPB + 1)]
    ss_.vres = [Res("vres_s%d" % i) for i in range(NPB + 1)]

    def mem_kv_prompt():
        g = Grp("mem", [0, 1], 128)
        for i in range(2):
            dma("sp", x_grp[:, i, :], memx[i * 128:(i + 1) * 128, :], (), [x_grp])
        norm_transpose(lambda i: x_grp[:, i, :], x_grp, g, gmT, hT)

        def cons(ct, i, ps, ncol):
            if ct < 2:
                kf = f5.next()
                headnorm(ps[:, 0:512], ps, 128, 512, 256, knm_t, kf[:, :], kf)
                dma("sp", mk_o[i * 128:(i + 1) * 128, ct * 512:(ct + 1) * 512], kf[:, :], [kf], ())
                kb_ = tb5.next()
                acopy(kb_[:, :], kf[:, :], [kf], [kb_])
                ps2 = psg.next()
                for q in range(4):
                    transpose_to(ps2, q * 128, kb_[:, q * 128:(q + 1) * 128], kb_, 128, 128)
                vcopy(memKT[:, 2 * ct:2 * ct + 2, :, i * 128:(i + 1) * 128],
                      ps2[:, 0:512].rearrange("p (h c t) -> p h c t", h=2, c=2), [ps2], [memKT])
            else:
                vf = f5.next()
                acopy(vf[:, :], ps[:, 0:512], [ps], [vf])
                dma("sp", mv_o[i * 128:(i + 1) * 128, (ct - 2) * 512:(ct - 1) * 512], vf[:, :], [vf], ())
                vcopy(memV[:, i, 2 * (ct - 2):2 * (ct - 2) + 2, :], vf[:, :].rearrange("p (h e) -> p h e", h=2),
                      [vf], [memV])
        proj_tok(hT, g, w_mem_kv, 0, 2048, cons)

    def sample_prep_block(pb):
        if True:
            kb_ = tokbf.next()
            dma("pool", kb_[:, :], ck[pb * 128:(pb + 1) * 128, :], (), [kb_])
            for half in range(2):
                ps = psg.next()
                for q in range(4):
                    h = half * 4 + q
                    transpose_to(ps, q * 128, kb_[:, h * 128:(h + 1) * 128], kb_, 128, 128)
                kt_ = kTblk.next()
                vcopy(kt_[:, :, :], ps[:, 0:512].rearrange("p (q t) -> p q t", q=4), [ps], [kt_])
                dma("sp", kT_s[half * 4:half * 4 + 4, :, pb * 128:(pb + 1) * 128].rearrange("h d t -> d h t"),
                    kt_[:, :, :], [kt_], [ss_.kres[pb]])
            vb_ = tokbf.next()
            dma("pool", vb_[:, :], cv[pb * 128:(pb + 1) * 128, :], (), [vb_])
            dma("sp", v_s[pb * 128:(pb + 1) * 128, :], vb_[:, :], [vb_], [ss_.vres[pb]])

    def sample_prep_mem():
        for mb in range(2):
            kb_ = tokbf.next()
            dma("pool", kb_[:, :], cmk[mb * 128:(mb + 1) * 128, :], (), [kb_])
            for half in range(2):
                ps = psg.next()
                for q in range(4):
                    kc = half * 4 + q
                    transpose_to(ps, q * 128, kb_[:, kc * 128:(kc + 1) * 128], kb_, 128, 128)
                vcopy(memKT[:, 2 * half:2 * half + 2, :, mb * 128:(mb + 1) * 128],
                      ps[:, 0:512].rearrange("p (h c t) -> p h c t", h=2, c=2), [ps], [memKT])
            dma("pool", memV[:, mb, :, :].rearrange("p h e -> p (h e)"), cmv[mb * 128:(mb + 1) * 128, :], (), [memV])

    def gla_stage(g, full):
        T, NB, NT = g.T, g.NB, g.NT
        wb = load_w(w_in, 0, KC, COL["gk"], 16)
        ps = psg.next()
        mm(ps[0:16, 0:NT], [(wb[:, kc, 0:16], hT[:, kc, 0:NT]) for kc in range(KC)], [wb, hT], [ps])
        vcopy(gkT[:, 0:NT], ps[0:16, 0:NT], [ps], [gkT])
        for i in range(NB):
            ps = psg.next()
            mm(ps[0:T, 0:512], [(gkT[0:16, i * T:(i + 1) * T], wgk2_t[0:16, :])], [gkT, wgk2_t], [ps])
            xg = f5.next()
            tt(xg[0:T, :], ps[0:T, 0:512], bgk_t[0:T, :], ALU.add, [ps, bgk_t], [xg])
            act(xg[0:T, :], xg[0:T, :], AF.Exp, [xg], [xg], scale=-1.0)
            act(Lst[0:T, i, :], xg[0:T, :], AF.Ln, [xg, one_t], [Lst], bias=one_t[0:T, :])

        def cons_k(ct, i, ps, ncol):
            acopy(Kst[0:T, i, :], ps[0:T, 0:512], [ps], [Kst])

        def cons_q(ct, i, ps, ncol):
            acopy(Qst[0:T, i, :], ps[0:T, 0:512], [ps], [Qst])

        def cons_v(ct, i, ps, ncol):
            vcopy(Vst[0:T, i, ct * 512:(ct + 1) * 512], ps[0:T, 0:512], [ps], [Vst])

        def cons_r(ct, i, ps, ncol):
            act(Rst[0:T, i, ct * 512:(ct + 1) * 512], ps[0:T, 0:512], AF.Silu, [ps], [Rst])

        proj_tok(hT, g, w_in, COL["kb"], 512, cons_k)
        if full:
            proj_tok(hT, g, w_in, COL["qb"], 512, cons_q)
        proj_tok(hT, g, w_in, COL["vb"], 1024, cons_v)
        if full:
            proj_tok(hT, g, w_in, COL["rb"], 1024, cons_r)

        obT = oT[0]
        if full:
            acopy(Sbf[:, :, :], Sst[:, :, :], [Sst], [Sbf])
        for i in range(NB):
            L_i = Lst[0:T, i, :]
            psE = psg.next()
            mm(psE[0:T, 0:512], [(trim_f[0:T, 0:T], L_i)], [trim_f, Lst], [psE])
            psD = psg.next()
            for h in range(4):
                mm(psD[:, h:h + 1], [(Lst[0:T, i, h * 128:(h + 1) * 128], ones_f[0:T, 0:1])], [Lst, ones_f], [psD])
            explb = gexpR.next()
            gdec = gdecR.next()
            gkh = gkhR.next()
            act(explb[0:T, :], psE[0:T, 0:512], AF.Exp, [psE], [explb], scale=1.0 / 16)
            act(gdec[:, 0:4], psD[:, 0:4], AF.Exp, [psD], [gdec], scale=-1.0 / 16)
            if full:
                psB = psg.next()
                mm(psB[0:T, 0:512], [(tri_f[0:T, 0:T], L_i)], [tri_f, Lst], [psB])
                expb = gexpR.next()
                expnb = gexpR.next()
                act(expb[0:T, :], psB[0:T, 0:512], AF.Exp, [psB], [expb], scale=-1.0 / 16)
                act(expnb[0:T, :], psB[0:T, 0:512], AF.Exp, [psB], [expnb], scale=1.0 / 16)
            tt(gkh[0:T, :], Kst[0:T, i, :], explb[0:T, :], ALU.mult, [Kst, explb], [gkh])
            if full:
                stt(gq[0:T, :], Qst[0:T, i, :], GLA_SCALE, expb[0:T, :], ALU.mult, ALU.mult, [Qst, expb], [gq])
                tt(gkt[0:T, :], Kst[0:T, i, :], expnb[0:T, :], ALU.mult, [Kst, expnb], [gkt])
                psQ = psg.next()
                for h in range(4):
                    transpose_to(psQ, h * T, gq[0:T, h * 128:(h + 1) * 128], gq, T, 128)
                acopy(gqT[:, :, 0:T], psQ[:, 0:4 * T].rearrange("p (h t) -> p h t", h=4), [psQ], [gqT])
                psK = psg.next()
                for h in range(4):
                    transpose_to(psK, h * T, gkt[0:T, h * 128:(h + 1) * 128], gkt, T, 128)
                vcopy(gkT2[:, :, 0:T], psK[:, 0:4 * T].rearrange("p (h t) -> p h t", h=4), [psK], [gkT2])
                psA = psg.next()
                for h in range(4):
                    mm(psA[0:T, h * T:(h + 1) * T], [(gkT2[:, h, 0:T], gqT[:, h, 0:T])], [gkT2, gqT], [psA])
                tt(gatt[0:T, :, 0:T], psA[0:T, 0:4 * T].rearrange("p (h t) -> p h t", h=4),
                   tri_f[0:T, 0:T].unsqueeze(1).to_broadcast([T, 4, T]), ALU.mult, [psA, tri_f], [gatt])
                for hb in range(2):
                    pso = psg.next()
                    for hh in range(2):
                        h = hb * 2 + hh
                        mm(pso[0:T, hh * 256:(hh + 1) * 256],
                           [(gqT[:, h, 0:T], Sbf[:, h, :]), (gatt[0:T, h, 0:T], Vst[0:T, i, h * 256:(h + 1) * 256])],
                           [gqT, Sbf, gatt, Vst], [pso])
                    ob = tb5.next()
                    headnorm(pso[0:T, 0:512], pso, T, 512, 256, slg_t, ob[0:T, :], ob,
                             extra=(Rst[0:T, i, hb * 512:(hb + 1) * 512], Rst))
                    pst = psg.next()
                    for q in range(4):
                        transpose_to(pst, q * T, ob[0:T, q * 128:(q + 1) * 128], ob, T, 128)
                    acopy(obT[:, hb * 4:hb * 4 + 4, i * T:(i + 1) * T],
                          pst[:, 0:4 * T].rearrange("p (q t) -> p q t", q=4), [pst], [obT])
            for hb in range(2):
                pss = psg.next()
                for hh in range(2):
                    h = hb * 2 + hh
                    mm(pss[:, hh * 256:(hh + 1) * 256],
                       [(gkh[0:T, h * 128:(h + 1) * 128], Vst[0:T, i, h * 256:(h + 1) * 256])], [gkh, Vst], [pss])
                for hh in range(2):
                    h = hb * 2 + hh
                    stt(Sst[:, h, :], Sst[:, h, :], gdec[:, h:h + 1], pss[:, hh * 256:(hh + 1) * 256],
                        ALU.mult, ALU.add, [Sst, gdec, pss], [Sst])
            if full:
                acopy(Sbf[:, :, :], Sst[:, :, :], [Sst], [Sbf])
        return obT

    def branch_proj(g, oTt, wp, gcol, first):
        T = g.T
        for ct in range(2):
            wg = load_w(w_in, 0, KC, gcol + ct * 512, 512)
            wpp = load_w(wp, 0, KC, ct * 512, 512)
            for i in range(g.NB):
                psa = psg.next()
                mm(psa[0:T, :], [(hT[:, kc, i * T:(i + 1) * T], wg[:, kc, :]) for kc in range(KC)], [hT, wg], [psa])
                psb = psg.next()
                mm(psb[0:T, :], [(oTt[:, kc, i * T:(i + 1) * T], wpp[:, kc, :]) for kc in range(KC)], [oTt, wpp], [psb])
                sg = f5.next()
                act(sg[0:T, :], psa[0:T, :], AF.Sigmoid, [psa], [sg])
                dst = m_acc[0:T, i, ct * 512:(ct + 1) * 512]
                if first:
                    tt(dst, sg[0:T, :], psb[0:T, :], ALU.mult, [sg, psb], [m_acc])
                else:
                    tt(sg[0:T, :], sg[0:T, :], psb[0:T, :], ALU.mult, [sg, psb], [sg])
                    tt(dst, dst, sg[0:T, :], ALU.add, [m_acc, sg], [m_acc])

    def diff_kv(g, seq, tok0_of, kres_of, outs):
        T = g.T

        def cons_k(ct, i, ps, ncol):
            kb_ = tb5.next()
            if outs is not None:
                kf = f5.next()
                headnorm(ps[0:T, 0:512], ps, T, 512, 64, kn_t, kf[0:T, :], kf)
                dma("sp", outs[0][outs[2](i):outs[2](i) + T, ct * 512:(ct + 1) * 512], kf[0:T, :], [kf], ())
                acopy(kb_[0:T, :], kf[0:T, :], [kf], [kb_])
            else:
                headnorm(ps[0:T, 0:512], ps, T, 512, 64, kn_t, kb_[0:T, :], kb_)

            def tail():
                ps2 = psg.next()
                for q in range(4):
                    transpose_to(ps2, q * T, kb_[0:T, q * 128:(q + 1) * 128], kb_, T, 128)
                kt_ = kTblk.next()
                vcopy(kt_[:, :, 0:T], ps2[:, 0:4 * T].rearrange("p (q t) -> p q t", q=4), [ps2], [kt_])
                t0 = tok0_of(i)
                dma("sp", seq.kT[ct * 4:ct * 4 + 4, :, t0:t0 + T].rearrange("h d t -> d h t"), kt_[:, :, 0:T],
                    [kt_], [seq.kres[kres_of(i)]])
            return tail

        def cons_v(ct, i, ps, ncol):
            t0 = tok0_of(i)
            if outs is not None:
                vf = f5.next()
                acopy(vf[0:T, :], ps[0:T, 0:512], [ps], [vf])
                dma("sp", outs[1][outs[2](i):outs[2](i) + T, ct * 512:(ct + 1) * 512], vf[0:T, :], [vf], ())
            vb_ = tb5.next()
            vcopy(vb_[0:T, :], ps[0:T, 0:512], [ps], [vb_])
            dma("sp", seq.v[t0:t0 + T, ct * 512:(ct + 1) * 512], vb_[0:T, :], [vb_], [seq.vres[kres_of(i)]])

        proj_tok(hT, g, w_in, COL["ka"], 1024, cons_k)
        proj_tok(hT, g, w_in, COL["va"], 1024, cons_v)

    def diff_q(g):
        T = g.T

        def cons_q(ct, i, ps, ncol):
            qb_ = tb5.next()
            headnorm(ps[0:T, 0:512], ps, T, 512, 64, qn_t, qb_[0:T, :], qb_)

            def tail():
                ps2 = psg.next()
                for q in range(4):
                    transpose_to(ps2, q * T, qb_[0:T, q * 128:(q + 1) * 128], qb_, T, 128)
                vcopy(QT[:, ct * 4:ct * 4 + 4, i * T:(i + 1) * T], ps2[:, 0:4 * T].rearrange("p (q t) -> p q t", q=4),
                      [ps2], [QT])
            return tail
        proj_tok(hT, g, w_in, COL["qa"], 1024, cons_q)

    def diff_attention(g, seq, keyblocks_of, oaT):
        T = g.T
        nq = g.NB
        QW = nq * T
        for v_ in Vp.items:
            vmemset(v_[:, :, 128:129], 1.0, [v_])
        kbs = keyblocks_of()

        def vis(qi, r):
            return r is None or qi >= r
        first_for, last_for = {}, {}
        for idx, kb in enumerate(kbs):
            for qi in range(nq):
                if vis(qi, kb[3]):
                    last_for[qi] = idx
                    first_for.setdefault(qi, idx)
        fin_tails = []
        fin_chain = []

        def flush_tails():
            while fin_tails:
                fin_tails.pop(0)()

        def flush_chain():
            while fin_chain:
                fin_chain.pop(0)()

        for h in range(8):
            ktp = vp = None
            pend_pv = [None]

            def do_pv(st, h=h):
                p_, vp_, j16_, r_, nk_, idx_ = st
                for qi in range(nq):
                    if not vis(qi, r_):
                        continue
                    for m in range(2):
                        mm(psO[qi][0:T, m * 129:(m + 1) * 129],
                           [(p_[0:nk_, m, qi * T:(qi + 1) * T], vp_[0:nk_, j16_, 0:129])], [p_, vp_], [psO[qi]],
                           start=(idx_ == first_for[qi] and m == 0), stop=(idx_ == last_for[qi]), skipgc=True)

            for idx, (kbi, tok0, nk, r, bias_ap, bias_res) in enumerate(kbs):
                if idx % PIECE == 0:
                    piece = kbs[idx:idx + PIECE]
                    ktp = KTp.next()
                    vp = Vp.next()
                    pt0 = piece[0][1]
                    ptn = sum(p[2] for p in piece)
                    kr = [seq.kres[p[0]] for p in piece]
                    vr = [seq.vres[p[0]] for p in piece]
                    dma("sp", ktp[:, 0:ptn], seq.kT[h, :, pt0:pt0 + ptn], kr, [ktp])
                    nfull = sum(1 for p in piece if p[2] == 128)
                    if nfull > 0:
                        dma("sp", vp[:, 0:nfull, 0:128],
                            seq.v[pt0:pt0 + nfull * 128, h * 128:(h + 1) * 128].rearrange("(b p) e -> p b e", p=128),
                            vr, [vp])
                    if nfull < len(piece):
                        pl = piece[-1]
                        dma("sp", vp[0:pl[2], nfull, 0:128], seq.v[pl[1]:pl[1] + pl[2], h * 128:(h + 1) * 128], vr, [vp])
                kof = tok0 - kbs[idx - idx % PIECE][1]
                j16 = idx % PIECE
                qa_ = 0 if r is None else r * T
                sset = psS.next()
                for m in range(2):
                    mm(sset[m][0:nk, qa_:QW],
                       [(ktp[m * 64:(m + 1) * 64, kof:kof + nk], QT[m * 64:(m + 1) * 64, h, qa_:QW])],
                       [ktp, QT], [sset[m]])
                p_ = Pt.next()
                act(p_[0:nk, :, qa_:QW], sset[2][0:nk, :].rearrange("p (m q) -> p m q", m=2)[:, :, qa_:QW], AF.Exp,
                    [sset[0], sset[1], bias_res], [p_], bias=bias_ap[0:nk, :], scale=DIFF_SCALE)
                if r is not None:
                    vmemset(p_[64:128, :, r * T:r * T + 64], 0.0, [p_])
                if pend_pv[0] is not None:
                    do_pv(pend_pv[0])
                pend_pv[0] = (p_, vp, j16, r, nk, idx)
                if idx == 4:
                    flush_chain()
                if idx == 28:
                    flush_tails()
            do_pv(pend_pv[0])
            flush_chain()
            flush_tails()
            osbs = []
            for qi in range(nq):
                osb = osbR.next()
                vcopy(osb[0:T, :], psO[qi][0:T, 0:258], [psO[qi]], [osb])
                osbs.append(osb)

            def fin(osbs=osbs, h=h):
                for qi in range(nq):
                    i = qi
                    po = osbs[qi]
                    rr = small.next()
                    ts(rr[0:T, 0:2], po[0:T, 0:258].rearrange("p (m e) -> p m e", m=2)[:, :, 128], 1e-30, None,
                       ALU.add, None, [po], [rr])
                    vrecip(rr[0:T, 0:2], rr[0:T, 0:2], [rr], [rr])
                    tt(rr[0:T, 1:2], rr[0:T, 1:2], nlam[0:T, :], ALU.mult, [rr, nlam], [rr])
                    o0 = oa0.next()
                    ts(o0[0:T, :], po[0:T, 0:128], rr[0:T, 0:1], None, ALU.mult, None, [po, rr], [o0])
                    o1 = oa1.next()
                    stt(o1[0:T, :], po[0:T, 129:257], rr[0:T, 1:2], o0[0:T, :], ALU.mult, ALU.add, [po, rr, o0], [o1])
                    tt(o0[0:T, :], o1[0:T, :], o1[0:T, :], ALU.mult, [o1], [o0])
                    s2 = small.next()
                    vreduce(s2[0:T, 0:1], o0[0:T, :], [o0], [s2])
                    ts(s2[0:T, 0:1], s2[0:T, 0:1], 1.0 / 128, EPS, ALU.mult, ALU.add, [s2], [s2])
                    tt(s2[0:T, 0:1], s2[0:T, 0:1], mhalf_t[0:T, :], ALU.pow, [s2, mhalf_t], [s2], eng="pool")
                    ob_ = oab.next()
                    stt(ob_[0:T, :], o1[0:T, :], s2[0:T, 0:1], sld_t[0:T, :], ALU.mult, ALU.mult, [o1, s2, sld_t], [ob_])

                    def ftail(ob_=ob_, h=h, i=i):
                        pst = psg.next()
                        transpose_to(pst, 0, ob_[0:T, :], ob_, T, 128)
                        vcopy(oaT[:, h, i * T:(i + 1) * T], pst[:, 0:T], [pst], [oaT])
                    fin_tails.append(ftail)
            fin_chain.append(fin)
        flush_chain()
        flush_tails()

    def diff_attention_hb(g, seq, keyblocks_of, oaT):
        T = g.T
        PH = 4
        psg.items = banks[0:4]
        for v_ in Vp.items:
            vmemset(v_[:, :, 128:129], 1.0, [v_])
        kbs = keyblocks_of()
        nkb = len(kbs)
        fin_tails = []
        fin_chain = []

        def flush(lst):
            while lst:
                lst.pop(0)()

        for hg in range(2):
            ktp = vp = None
            pend_pv = [None]

            def do_pv(st):
                p_, vp_, j_, nk_, idx_ = st
                vf_ = vp_[:, :, :].rearrange("p s e -> p (s e)")
                for hs in range(4):
                    for m in range(2):
                        lh = p_[0:nk_, m, hs * T:(hs + 1) * T]
                        mm(psO[hs][0:T, m * 129:m * 129 + 128],
                           [(lh, vf_[0:nk_, j_ * 512 + hs * 128:j_ * 512 + (hs + 1) * 128])], [p_, vp_], [psO[hs]],
                           start=(idx_ == 0 and m == 0), stop=(idx_ == nkb - 1), skipgc=True)
                        mm(psO[hs][0:T, m * 129 + 128:m * 129 + 129], [(lh, ones_b[0:nk_, 0:1])], [p_, ones_b], [psO[hs]],
                           start=False, stop=(idx_ == nkb - 1), skipgc=True)

            for idx, (kbi, tok0, nk, r, bias_ap, bias_res) in enumerate(kbs):
                if idx % PH == 0:
                    piece = kbs[idx:idx + PH]
                    ktp = KTp.next()
                    vp = Vp.next()
                    pt0 = piece[0][1]
                    ptn = sum(p[2] for p in piece)
                    kr = [seq.kres[p[0]] for p in piece]
                    vr = [seq.vres[p[0]] for p in piece]
                    nfull = sum(1 for p in piece if p[2] == 128)
                    dma("sp", ktp[:, :].rearrange("p (h t) -> p h t", h=4)[:, :, 0:ptn],
                        seq.kT[hg * 4:hg * 4 + 4, :, pt0:pt0 + ptn].rearrange("h d t -> d h t"), kr, [ktp])
                    vflat = vp[:, :, :].rearrange("p s e -> p (s e)")
                    if nfull > 0:
                        dma("sp", vflat[:, 0:nfull * 512].rearrange("p (b c) -> p b c", c=512),
                            seq.v[pt0:pt0 + nfull * 128, hg * 512:(hg + 1) * 512].rearrange("(b p) c -> p b c", p=128),
                            vr, [vp])
                    if nfull < len(piece):
                        pl = piece[-1]
                        dma("sp", vflat[0:pl[2], nfull * 512:(nfull + 1) * 512],
                            seq.v[pl[1]:pl[1] + pl[2], hg * 512:(hg + 1) * 512], vr, [vp])
                kof = tok0 - kbs[idx - idx % PH][1]
                j_ = idx % PH
                sset = psS.next()
                for hs in range(4):
                    h = hg * 4 + hs
                    for m in range(2):
                        mm(sset[m][0:nk, hs * T:(hs + 1) * T],
                           [(ktp[m * 64:(m + 1) * 64, hs * 512 + kof:hs * 512 + kof + nk], QT[m * 64:(m + 1) * 64, h, 0:T])],
                           [ktp, QT], [sset[m]])
                p_ = Pt.next()
                act(p_[0:nk, :, 0:4 * T], sset[2][0:nk, :].rearrange("p (m q) -> p m q", m=2)[:, :, 0:4 * T], AF.Exp,
                    [sset[0], sset[1], bias_res], [p_], bias=bias_ap[0:nk, :], scale=DIFF_SCALE)
                if r is not None:
                    vmemset(p_[64:128, :, 0:4 * T].rearrange("p m (h t) -> p m h t", h=4)[:, :, :, 0:64], 0.0, [p_])
                if pend_pv[0] is not None:
                    do_pv(pend_pv[0])
                pend_pv[0] = (p_, vp, j_, nk, idx)
                if idx == 4:
                    flush(fin_chain)
                if idx == 20:
                    flush(fin_tails)
            do_pv(pend_pv[0])
            flush(fin_chain)
            flush(fin_tails)
            osbs = []
            for hs in range(4):
                osb = osbR.next()
                vcopy(osb[0:T, :], psO[hs][0:T, 0:258], [psO[hs]], [osb])
                osbs.append(osb)

            def fin(osbs=osbs, hg=hg):
                for hs in range(4):
                    h = hg * 4 + hs
                    po = osbs[hs]
                    rr = small.next()
                    ts(rr[0:T, 0:2], po[0:T, 0:258].rearrange("p (m e) -> p m e", m=2)[:, :, 128], 1e-30, None,
                       ALU.add, None, [po], [rr])
                    vrecip(rr[0:T, 0:2], rr[0:T, 0:2], [rr], [rr])
                    tt(rr[0:T, 1:2], rr[0:T, 1:2], nlam[0:T, :], ALU.mult, [rr, nlam], [rr])
                    o0 = oa0.next()
                    ts(o0[0:T, :], po[0:T, 0:128], rr[0:T, 0:1], None, ALU.mult, None, [po, rr], [o0])
                    o1 = oa1.next()
                    stt(o1[0:T, :], po[0:T, 129:257], rr[0:T, 1:2], o0[0:T, :], ALU.mult, ALU.add, [po, rr, o0], [o1])
                    tt(o0[0:T, :], o1[0:T, :], o1[0:T, :], ALU.mult, [o1], [o0])
                    s2 = small.next()
                    vreduce(s2[0:T, 0:1], o0[0:T, :], [o0], [s2])
                    ts(s2[0:T, 0:1], s2[0:T, 0:1], 1.0 / 128, EPS, ALU.mult, ALU.add, [s2], [s2])
                    tt(s2[0:T, 0:1], s2[0:T, 0:1], mhalf_t[0:T, :], ALU.pow, [s2, mhalf_t], [s2], eng="pool")
                    ob_ = oab.next()
                    stt(ob_[0:T, :], o1[0:T, :], s2[0:T, 0:1], sld_t[0:T, :], ALU.mult, ALU.mult, [o1, s2, sld_t], [ob_])

                    def ftail(ob_=ob_, h=h):
                        pst = psg.next()
                        transpose_to(pst, 0, ob_[0:T, :], ob_, T, 128)
                        vcopy(oaT[:, h, 0:T], pst[:, 0:T], [pst], [oaT])
                    fin_tails.append(ftail)
            fin_chain.append(fin)
        flush(fin_chain)
        flush(fin_tails)
        psg.items = banks[0:4]

    def mem_stage(g, omT):
        T, NT = g.T, g.NT

        def cons_q(ct, i, ps, ncol):
            qb_ = tb5.next()
            headnorm(ps[0:T, 0:512], ps, T, 512, 256, qnm_t, qb_[0:T, :], qb_)

            def tail():
                ps2 = psg.next()
                for q in range(4):
                    transpose_to(ps2, q * T, qb_[0:T, q * 128:(q + 1) * 128], qb_, T, 128)
                vcopy(QMT[:, ct * 4:ct * 4 + 4, i * T:(i + 1) * T], ps2[:, 0:4 * T].rearrange("p (q t) -> p q t", q=4),
                      [ps2], [QMT])
            return tail
        proj_tok(hT, g, w_in, COL["qm"], 1024, cons_q)
        for h in range(4):
            pm_ = pm.next()
            for mc in range(2):
                ps = psg.next()
                mm(ps[:, 0:NT], [(memKT[:, h, c, mc * 128:(mc + 1) * 128], QMT[:, 2 * h + c, 0:NT]) for c in range(2)],
                   [memKT, QMT], [ps])
                act(pm_[:, mc, 0:NT], ps[:, 0:NT], AF.Exp, [ps], [pm_], scale=MEM_SCALE)
            psl = psg.next()
            mm(psl[:, 0:NT], [(ones_b[:, :], pm_[:, mc, 0:NT]) for mc in range(2)], [ones_b, pm_], [psl])
            rl_ = rl.next()
            vrecip(rl_[:, 0:NT], psl[:, 0:NT], [psl], [rl_])
            for ec in range(2):
                pso = psg.next()
                mm(pso[:, 0:NT], [(memV[:, mc, h, ec * 128:(ec + 1) * 128], pm_[:, mc, 0:NT]) for mc in range(2)],
                   [memV, pm_], [pso])
                tt(omT[:, 2 * h + ec, 0:NT], pso[:, 0:NT], rl_[:, 0:NT], ALU.mult, [pso, rl_], [omT])

    def out_stage(g):
        T = g.T
        mT = oT[1]
        for i in range(g.NB):
            mb_ = tokbf.next()
            acopy(mb_[0:T, :], m_acc[0:T, i, :], [m_acc], [mb_])
            for half in range(2):
                ps = psg.next()
                for q in range(4):
                    kc = half * 4 + q
                    transpose_to(ps, q * T, mb_[0:T, kc * 128:(kc + 1) * 128], mb_, T, 128)
                vcopy(mT[:, half * 4:half * 4 + 4, i * T:(i + 1) * T], ps[:, 0:4 * T].rearrange("p (q t) -> p q t", q=4),
                      [ps], [mT])
        for ct in range(2):
            wb = load_w(w_out, 0, KC, ct * 512, 512)
            for i in range(g.NB):
                ps = psg.next()
                mm(ps[0:T, :], [(mT[:, kc, i * T:(i + 1) * T], wb[:, kc, :]) for kc in range(KC)], [mT, wb], [ps])
                xs_ = x_grp[0:T, i, ct * 512:(ct + 1) * 512]
                tt(xs_, xs_, ps[0:T, :], ALU.add, [x_grp, ps], [x_grp])

    def ffn_stage(g, y_dst, halo):
        T, NT = g.T, g.NT
        carry = cur_carry[0]
        norm_transpose(lambda i: x_grp[0:T, i, :], x_grp, g, gfT, hT)
        for t6 in range(6):
            ncol = 512 if t6 < 5 else 256
            wu = load_w(w_up, 0, KC, t6 * 512, ncol)
            wv = None if halo else load_w(w_up, 0, KC, DFF + t6 * 512, ncol)
            for f4 in range(ncol // 128):
                f = t6 * 4 + f4
                psu = psg.next()
                mm(psu[:, 0:NT], [(wu[:, kc, f4 * 128:(f4 + 1) * 128], hT[:, kc, 0:NT]) for kc in range(KC)],
                   [wu, hT], [psu])
                if halo:
                    ts(carry[:, f, :], psu[:, NT - 2:NT], flag[:, 0:1], None, ALU.mult, None, [psu, flag], [carry])
                    continue
                psv = psg.next()
                mm(psv[:, 0:NT], [(wv[:, kc, f4 * 128:(f4 + 1) * 128], hT[:, kc, 0:NT]) for kc in range(KC)],
                   [wv, hT], [psv])
                u_ = ub.next()
                vcopy(u_[:, 0:2], carry[:, f, :], [carry], [u_])
                acopy(u_[:, 2:2 + NT], psu[:, 0:NT], [psu], [u_])
                t1 = t1b.next()
                ts(t1[:, 0:NT], u_[:, 2:2 + NT], cwT[:, 2, f:f + 1], cbT[:, f:f + 1], ALU.mult, ALU.add,
                   [u_, cwT, cbT], [t1])
                stt(t1[:, 0:NT], u_[:, 1:1 + NT], cwT[:, 1, f:f + 1], t1[:, 0:NT], ALU.mult, ALU.add, [u_, cwT, t1], [t1])
                stt(t1[:, 0:NT], u_[:, 0:NT], cwT[:, 0, f:f + 1], t1[:, 0:NT], ALU.mult, ALU.add, [u_, cwT, t1], [t1])
                act(t1[:, 0:NT], t1[:, 0:NT], AF.Gelu_apprx_tanh, [t1], [t1])
                tt(actT[:, f, 0:NT], t1[:, 0:NT], psv[:, 0:NT], ALU.mult, [t1, psv], [actT])
                vcopy(carry[:, f, :], u_[:, NT:NT + 2], [u_], [carry])
        if halo:
            return
        parts = [(0, 8), (8, 8), (16, 6)]
        for ct in range(2):
            for pi, (f0, nk) in enumerate(parts):
                wb = load_w(w_down, f0 * 128, nk, ct * 512, 512)
                for i in range(g.NB):
                    acc = banks[i]
                    mm(acc[0:T, :], [(actT[:, f0 + k, i * T:(i + 1) * T], wb[:, k, :]) for k in range(nk)],
                       [actT, wb], [acc], start=(pi == 0), stop=(pi == 2))
            for i in range(g.NB):
                yo = f5.next()
                tt(yo[0:T, :], x_grp[0:T, i, ct * 512:(ct + 1) * 512], banks[i][0:T, :], ALU.add, [x_grp, banks[i]], [yo])
                r0 = y_dst[1](i)
                dma("sp", y_dst[0][r0:r0 + T, ct * 512:(ct + 1) * 512], yo[0:T, :], [yo], ())

    xloaded = [False]
    next_load = [None]
    def run_group(g, seq, x_src_of, tok0_of, kres_of, keyblocks_of, outs_kv, y_dst):
        T = g.T
        full = g.kind != "prefix"
        if not xloaded[0]:
            for i in range(g.NB):
                dma("sp", x_grp[0:T, i, :], x_src_of(i), (), [x_grp])
        xloaded[0] = False
        norm_transpose(lambda i: x_grp[0:T, i, :], x_grp, g, gaT, hT)
        if not full and next_load[0] is not None:
            next_load[0]()
            xloaded[0] = True
        ckpt(g.kind + " norm")
        obT = gla_stage(g, full)
        ckpt(g.kind + " gla")
        if full:
            branch_proj(g, obT, w_pg, COL["gb"], True)
            ckpt(g.kind + " proj_gla")
        diff_kv(g, seq, tok0_of, kres_of, outs_kv)
        ckpt(g.kind + " diff_kv")
        if not full:
            return
        diff_q(g)
        ckpt(g.kind + " diff_q")
        oaT = oT[1]
        if g.NB == 1:
            diff_attention_hb(g, seq, keyblocks_of, oaT)
        else:
            diff_attention(g, seq, keyblocks_of, oaT)
        ckpt(g.kind + " diff_attn")
        branch_proj(g, oaT, w_pd, COL["ga"], False)
        omT = oT[0]
        mem_stage(g, omT)
        ckpt(g.kind + " mem")
        branch_proj(g, omT, w_pm, COL["gm"], False)
        out_stage(g)
        ckpt(g.kind + " out")
        ffn_stage(g, y_dst, g.kind == "halo")
        ckpt(g.kind + " ffn")

    stage_no = [0]

    def ckpt(name):
        stage_no[0] += 1
        if STOP_AFTER is not None and stage_no[0] >= STOP_AFTER:
            if not S.stopped:
                print("STOP after stage", stage_no[0], name)
            S.stopped = True

    mem_kv_prompt()
    ckpt("mem_kv")
    sprep_next = [0]

    def sample_prep_some(n):
        while n > 0 and sprep_next[0] < NPB:
            sample_prep_block(sprep_next[0])
            sprep_next[0] += 1
            n -= 1
    vmemset(Sst[:, :, :], 0.0, [Sst])
    vmemset(Sbf[:, :, :], 0.0, [Sbf])
    vmemset(carry_p[:, :, :], 0.0, [carry_p])
    for f in range(NFC):
        dma("sp", carry_s[:, f, :], sconv[:, f * 128:(f + 1) * 128].rearrange("t p -> p t"), (), [carry_s], slow=True)

    def prompt_keyblocks(g):
        def f():
            wb0 = g.blocks[0]
            out = []
            for kb in range(0, g.blocks[-1] + 1):
                out.append((kb, kb * 128, 128, None if kb < wb0 else kb - wb0, kbias[:, kb:kb + 1], kbias))
            return out
        return f

    groups = []
    b = 0
    while b < NPRE:
        groups.append(Grp("prefix", list(range(b, min(b + 4, NPRE))), 128))
        b += 4
    groups.append(Grp("halo", [NPRE], 128))
    b = NPRE + 1
    while b < NWIN:
        groups.append(Grp("own", list(range(b, min(b + 4, NWIN))), 128))
        b += 4
    for gi, g in enumerate(groups):
        own0 = NPRE + 1
        conv_on[0] = (g.kind == "prefix")
        next_load[0] = None
        if gi + 1 < len(groups):
            def _nl(g2=groups[gi + 1]):
                for i2 in range(g2.NB):
                    dma("sp", x_grp[0:g2.T, i2, :], xw[g2.blocks[i2] * 128:(g2.blocks[i2] + 1) * 128, :], (), [x_grp])
            next_load[0] = _nl
        outs_kv = None
        y_dst = None
        if g.kind == "own":
            outs_kv = (dk_o, dv_o, (lambda g_: (lambda i: (g_.blocks[i] - own0) * 128))(g))
            y_dst = (y_o, (lambda g_: (lambda i: (g_.blocks[i] - own0) * 128))(g))
        run_group(g, sp_,
                  (lambda g_: (lambda i: xw[g_.blocks[i] * 128:(g_.blocks[i] + 1) * 128, :]))(g),
                  (lambda g_: (lambda i: g_.blocks[i] * 128))(g),
                  (lambda g_: (lambda i: g_.blocks[i]))(g),
                  prompt_keyblocks(g), outs_kv, y_dst)
        if g.kind == "prefix":
            sample_prep_some(2)
            if gi + 1 < len(groups) and groups[gi + 1].kind != "prefix":
                convert_some(4 * NWT)
    sample_prep_some(NPB)
    dma("sp", gs_o.rearrange("(h d) v -> d h v", h=4), Sst[:, :, :], [Sst], ())

    sample_prep_mem()
    dma("sp", Sst[:, :, :], sgla.rearrange("(h d) v -> d h v", h=4), (), [Sst])
    acopy(Sbf[:, :, :], Sst[:, :, :], [Sst], [Sbf])
    cur_carry[0] = carry_s
    gsmp = Grp("sample", [NPB], SD)
    conv_on[0] = False
    next_load[0] = None
    dma("sp", x_grp[0:SD, 0, :], xs[0:SD, :], (), [x_grp])
    xloaded[0] = True
    for f in range(NFC):
        dma("sp", cs_o[:, f * 128:(f + 1) * 128].rearrange("t p -> p t"), carry_p[:, f, :], [carry_p], (), slow=True)

    def sample_keyblocks():
        out = [(kb, kb * 128, 128, None, zero_t[:, 0:1], zero_t) for kb in range(NPB)]
        out.append((NPB, PAST, SD, None, zero_t[:, 0:1], zero_t))
        return out

    run_group(gsmp, ss_, lambda i: xs[0:SD, :], lambda i: PAST, lambda i: NPB, sample_keyblocks,
              (dks_o, dvs_o, lambda i: 0), (ys_o, lambda i: 0))
    dma("sp", gss_o.rearrange("(h d) v -> d h v", h=4), Sst[:, :, :], [Sst], ())
    for f in range(NFC):
        dma("sp" if f % 2 == 0 else "pool", css_o[:, f * 128:(f + 1) * 128].rearrange("t p -> p t"), carry_s[:, f, :],
            [carry_s], (), slow=True)

    S.emit()
    print("ops per engine:", S.stats, "waits:", S.nwaits)
    return nc


_CACHE = {}


def _prep_inputs(inp):
    xp = np.asarray(inp["x_prompt"], np.float32)
    B, SEQ, _ = xp.shape
    xs = np.asarray(inp["x_sample"], np.float32)
    DB, SD, _ = xs.shape
    PAST = inp["cache_diff_k"].shape[2]
    NWIN = SEQ // 128
    OWN = SEQ // 4
    ident = np.eye(128, dtype=np.float32)
    tri = np.triu(np.ones((128, 128), np.float32))
    shared = {"ident": ident, "tri": tri}
    for k in ("g_attn", "w_in", "w_gk2", "b_gk", "qn_diff", "kn_diff", "lam_q1", "lam_k1", "lam_q2", "lam_k2",
              "subln_diff", "subln_gla", "g_mem", "w_mem_kv", "qn_mem", "kn_mem", "w_proj_diff", "w_proj_gla",
              "w_proj_mem", "w_out", "g_ffn", "w_up", "conv_w", "conv_b", "w_down"):
        a = np.asarray(inp[k], np.float32)[0]
        if a.ndim == 1:
            a = a[None, :]
        shared[k] = np.ascontiguousarray(a)
    maps = []
    for c in range(8):
        n, j = c // 4, c % 4
        end = OWN * (j + 1)
        start = end - SEQ
        xw = np.zeros((SEQ, D), np.float32)
        xw[max(0, -start):] = xp[n, max(0, start):end]
        kb = np.zeros((128, NWIN), np.float32)
        npad = max(0, -start) // 128
        kb[:, :npad] = NEG
        m = dict(shared)
        m["xw"] = xw
        m["kbias"] = kb
        m["flag"] = np.full((128, 1), 1.0 if j > 0 else 0.0, np.float32)
        m["xs"] = np.ascontiguousarray(xs[c])
        m["ck"] = np.ascontiguousarray(np.asarray(inp["cache_diff_k"], np.float32)[0, c].reshape(PAST, D))
        m["cv"] = np.ascontiguousarray(np.asarray(inp["cache_diff_v"], np.float32)[0, c].reshape(PAST, D))
        m["cmk"] = np.ascontiguousarray(np.asarray(inp["cache_mem_k"], np.float32)[0, c].reshape(256, D))
        m["cmv"] = np.ascontiguousarray(np.asarray(inp["cache_mem_v"], np.float32)[0, c].reshape(256, D))
        m["sgla"] = np.ascontiguousarray(np.asarray(inp["state_gla"], np.float32)[0, c].reshape(512, 256))
        m["sconv"] = np.ascontiguousarray(np.asarray(inp["state_conv"], np.float32)[0, c])
        m["mem"] = np.ascontiguousarray(np.asarray(inp["mem_prompt"], np.float32)[n])
        maps.append(m)
    return maps, (B, SEQ, DB, SD, PAST, NWIN, OWN)


def kernel(**inp):
    maps, (B, SEQ, DB, SD, PAST, NWIN, OWN) = _prep_inputs(inp)
    key = (NWIN, PAST, SD)
    if key not in _CACHE:
        _CACHE[key] = build(NWIN, PAST, SD)
    nc = _CACHE[key]
    res = run_bass_kernel_spmd(nc, maps, core_ids=list(range(8)))
    R = res.results
    y = np.zeros((B, SEQ, D), np.float32)
    dk = np.zeros((1, B, SEQ, 8, 2, 64), np.float32)
    dv = np.zeros((1, B, SEQ, 8, 128), np.float32)
    mk = np.zeros((1, B, 256, 4, 256), np.float32)
    mv = np.zeros((1, B, 256, 4, 256), np.float32)
    gs = np.zeros((1, B, 4, 128, 256), np.float32)
    cs = np.zeros((1, B, 2, DFF), np.float32)
    ys = np.zeros((DB, SD, D), np.float32)
    dks = np.zeros((1, DB, SD, 8, 2, 64), np.float32)
    dvs = np.zeros((1, DB, SD, 8, 128), np.float32)
    gss = np.zeros((1, DB, 4, 128, 256), np.float32)
    css = np.zeros((1, DB, 2, DFF), np.float32)
    for c in range(8):
        n, j = c // 4, c % 4
        r = R[c]
        sl = slice(OWN * j, OWN * (j + 1))
        y[n, sl] = r["y"]
        dk[0, n, sl] = r["dk"].reshape(OWN, 8, 2, 64)
        dv[0, n, sl] = r["dv"].reshape(OWN, 8, 128)
        if j == 0:
            mk[0, n] = r["mk"].reshape(256, 4, 256)
            mv[0, n] = r["mv"].reshape(256, 4, 256)
        if j == 3:
            gs[0, n] = r["gs"].reshape(4, 128, 256)
            cs[0, n] = r["cs"]
        ys[c] = r["ys"]
        dks[0, c] = r["dks"].reshape(SD, 8, 2, 64)
        dvs[0, c] = r["dvs"].reshape(SD, 8, 128)
        gss[0, c] = r["gss"].reshape(4, 128, 256)
        css[0, c] = r["css"]
    return (y, ys, dk, dv, mk, mv, gs, cs, dks, dvs, gss, css)
```
